# Optimizing a Trainium2 kernel written in Bass

```python
import math
import jax, jax.numpy as jnp
from jax import lax
import numpy as np

D_MODEL = 1024
BATCH = 8
SEQ = 4096
DEPTH = 4

CHUNK = 64

N_B_LAYERS = DEPTH // 2
N_A_LAYERS = DEPTH - N_B_LAYERS

SSM_EXPAND = 2
SSM_D_INNER = SSM_EXPAND * D_MODEL
SSM_HEAD_DIM = 64
SSM_HEADS = SSM_D_INNER // SSM_HEAD_DIM
SSM_GROUPS = 8
SSM_HEADS_PER_GROUP = SSM_HEADS // SSM_GROUPS
SSM_STATE = 128
SSM_CONV = 4
SSM_CONV_DIM = SSM_D_INNER + 2 * SSM_GROUPS * SSM_STATE
SSM_IN_DIM = 2 * SSM_D_INNER + 2 * SSM_GROUPS * SSM_STATE + SSM_HEADS
SSD_CHUNK = CHUNK
DT_MIN = 0.001
DT_MAX = 0.1

DIFF_HEAD_DIM = 64
DIFF_V_DIM = 2 * DIFF_HEAD_DIM
DIFF_HEADS = D_MODEL // DIFF_V_DIM
DIFF_WIDTH = DIFF_HEADS * DIFF_V_DIM
KV_DIM = DIFF_HEADS * 2 * DIFF_HEAD_DIM + DIFF_WIDTH
Q_BLOCK = 128
ALIBI_MAX_EXP = 8.0

EPS = 1e-5

kernel_name = "yoco_mamba2_diffattn_alibi_trunk"


def rms_norm(x, g):
    xf = x.astype(jnp.float32)
    y = xf * lax.rsqrt(jnp.mean(xf * xf, axis=-1, keepdims=True) + EPS)
    return (y * g.astype(jnp.float32)).astype(x.dtype)


def gated_group_rms_norm(y, z, g):
    u = y.astype(jnp.float32) * jax.nn.silu(z.astype(jnp.float32))
    shp = u.shape
    u = u.reshape(shp[:-1] + (SSM_GROUPS, SSM_D_INNER // SSM_GROUPS))
    u = u * lax.rsqrt(jnp.mean(u * u, axis=-1, keepdims=True) + EPS)
    return u.reshape(shp) * g.astype(jnp.float32)


def causal_depthwise_conv(u, w, bias):
    c = u.shape[-1]
    out = lax.conv_general_dilated(
        u, w[:, None, :].astype(u.dtype), window_strides=(1,),
        padding=[(SSM_CONV - 1, 0)], dimension_numbers=("NWC", "WIO", "NWC"),
        feature_group_count=c)
    return out + bias.astype(u.dtype)


def ssd_chunked_scan(xdt, adt, bm, cm):
    b, s = xdt.shape[:2]
    nc = s // SSD_CHUNK

    def to_chunks(t):
        return jnp.moveaxis(t.reshape((b, nc, SSD_CHUNK) + t.shape[2:]), 1, 0)

    causal = jnp.tril(jnp.ones((SSD_CHUNK, SSD_CHUNK), dtype=bool))[None, :, :, None, None]

    def step(state, inp):
        x_c, a_c, b_c, c_c = inp
        acs = jnp.cumsum(a_c, axis=1)
        seg = acs[:, :, None] - acs[:, None, :]
        decay = jnp.exp(jnp.where(causal, seg, -jnp.inf))
        cb = jnp.einsum("blgn,bsgn->blsg", c_c, b_c)
        y_diag = jnp.einsum("blsgr,bsgrp->blgrp", cb[..., None] * decay, x_c)
        y_off = jnp.einsum("blgn,bgrpn->blgrp", c_c, state) * jnp.exp(acs)[..., None]
        to_end = jnp.exp(acs[:, -1:] - acs)
        new_state = (state * jnp.exp(acs[:, -1])[..., None, None]
                     + jnp.einsum("blgn,blgrp->bgrpn", b_c, x_c * to_end[..., None]))
        return new_state, y_diag + y_off

    state0 = jnp.zeros((b, SSM_GROUPS, SSM_HEADS_PER_GROUP, SSM_HEAD_DIM, SSM_STATE), jnp.float32)
    _, ys = lax.scan(step, state0, (to_chunks(xdt), to_chunks(adt), to_chunks(bm), to_chunks(cm)))
    return jnp.moveaxis(ys, 0, 1).reshape(xdt.shape)


def mamba2_layer(x, norm_g, w_in, conv_w, conv_b, dt_bias, a_log, d_skip, gate_norm_g, w_out):
    b, s, _ = x.shape
    G, R, P, N = SSM_GROUPS, SSM_HEADS_PER_GROUP, SSM_HEAD_DIM, SSM_STATE
    h = rms_norm(x, norm_g)
    zxbcdt = h @ w_in
    z = zxbcdt[..., :SSM_D_INNER]
    xbc = zxbcdt[..., SSM_D_INNER:SSM_D_INNER + SSM_CONV_DIM]
    dt_raw = zxbcdt[..., SSM_D_INNER + SSM_CONV_DIM:]
    xbc = jax.nn.silu(causal_depthwise_conv(xbc, conv_w, conv_b))
    gn = G * N
    xs = xbc[..., :SSM_D_INNER].reshape(b, s, G, R, P).astype(jnp.float32)
    bm = xbc[..., SSM_D_INNER:SSM_D_INNER + gn].reshape(b, s, G, N).astype(jnp.float32)
    cm = xbc[..., SSM_D_INNER + gn:].reshape(b, s, G, N).astype(jnp.float32)
    dt = jax.nn.softplus(dt_raw.astype(jnp.float32) + dt_bias.astype(jnp.float32)).reshape(b, s, G, R)
    a = -jnp.exp(a_log.astype(jnp.float32)).reshape(G, R)
    y = ssd_chunked_scan(xs * dt[..., None], a * dt, bm, cm)
    y = y + d_skip.astype(jnp.float32).reshape(G, R, 1) * xs
    y = gated_group_rms_norm(y.reshape(b, s, SSM_D_INNER), z, gate_norm_g).astype(x.dtype)
    return x + y @ w_out


def shared_kv(x, kv_norm_g, w_kv):
    b, s, _ = x.shape
    kv = rms_norm(x, kv_norm_g) @ w_kv
    k = kv[..., :DIFF_HEADS * 2 * DIFF_HEAD_DIM].reshape(b, s, DIFF_HEADS, 2, DIFF_HEAD_DIM)
    v = kv[..., DIFF_HEADS * 2 * DIFF_HEAD_DIM:].reshape(b, s, DIFF_HEADS, DIFF_V_DIM)
    return k.transpose(0, 2, 3, 1, 4), v.transpose(0, 2, 1, 3)


def diff_attention_layer(x, k, v, norm_g, w_in, lam_qk, sub_g, w_out, lambda_init):
    b, s, _ = x.shape
    H, d = DIFF_HEADS, DIFF_HEAD_DIM
    nblk = s // Q_BLOCK
    qg = rms_norm(x, norm_g) @ w_in
    q = qg[..., :DIFF_WIDTH].reshape(b, nblk, Q_BLOCK, H, 2, d).transpose(1, 0, 3, 4, 2, 5)
    gate = qg[..., DIFF_WIDTH:]
    lf = lam_qk.astype(jnp.float32)
    lam = jnp.exp(jnp.sum(lf[0] * lf[1])) - jnp.exp(jnp.sum(lf[2] * lf[3])) + lambda_init
    slopes = 2.0 ** (-ALIBI_MAX_EXP * jnp.arange(1, H + 1, dtype=jnp.float32) / H)
    key_pos = jnp.arange(s)
    scale = 1.0 / math.sqrt(d)

    def attend_block(args):
        qb, blk = args
        qpos = blk * Q_BLOCK + jnp.arange(Q_BLOCK)
        dist = jnp.abs(qpos[:, None] - key_pos[None, :]).astype(jnp.float32)
        allowed = (key_pos // CHUNK)[None, :] <= (qpos // CHUNK)[:, None]
        scores = jnp.einsum("bhiqd,bhikd->bhiqk", qb, k).astype(jnp.float32) * scale
        scores = scores - slopes[None, :, None, None, None] * dist
        probs = jax.nn.softmax(jnp.where(allowed, scores, -jnp.inf), axis=-1)
        weights = probs[:, :, 0] - lam * probs[:, :, 1]
        return jnp.einsum("bhqk,bhke->bqhe", weights.astype(v.dtype), v)

    out = lax.map(attend_block, (q, jnp.arange(nblk)))
    out = jnp.moveaxis(out, 0, 1).reshape(b, s, H, DIFF_V_DIM)
    out = (rms_norm(out, sub_g) * (1.0 - lambda_init)).reshape(b, s, DIFF_WIDTH)
    return x + (out * jax.nn.silu(gate)) @ w_out


def setup_inputs(seed: int = 0) -> dict:
    key = jax.random.key(seed)
    ks = jax.random.split(key, 18)
    f32 = jnp.float32

    def nrm(k, shape, scale):
        return jax.random.normal(k, shape, f32) * scale

    def gain(k, shape):
        return 1.0 + 0.05 * jax.random.normal(k, shape, f32)

    x = jax.random.normal(ks[0], (BATCH, SEQ, D_MODEL), f32)
    a_norm_g = gain(ks[1], (N_A_LAYERS, D_MODEL))
    a_w_in = nrm(ks[2], (N_A_LAYERS, D_MODEL, SSM_IN_DIM), D_MODEL ** -0.5)
    a_conv_w = nrm(ks[3], (N_A_LAYERS, SSM_CONV, SSM_CONV_DIM), SSM_CONV ** -0.5)
    a_conv_b = nrm(ks[4], (N_A_LAYERS, SSM_CONV_DIM), 0.01)
    u = jax.random.uniform(ks[5], (N_A_LAYERS, SSM_HEADS), f32)
    dt0 = jnp.exp(u * (math.log(DT_MAX) - math.log(DT_MIN)) + math.log(DT_MIN))
    a_dt_bias = dt0 + jnp.log(-jnp.expm1(-dt0))
    a_a_log = jnp.log(jax.random.uniform(ks[6], (N_A_LAYERS, SSM_HEADS), f32, 1.0, 16.0))
    a_d_skip = gain(ks[7], (N_A_LAYERS, SSM_HEADS))
    a_gate_norm_g = gain(ks[8], (N_A_LAYERS, SSM_D_INNER))
    a_w_out = nrm(ks[9], (N_A_LAYERS, SSM_D_INNER, D_MODEL), SSM_D_INNER ** -0.5)
    kv_norm_g = gain(ks[10], (D_MODEL,))
    w_kv = nrm(ks[11], (D_MODEL, KV_DIM), D_MODEL ** -0.5)
    b_norm_g = gain(ks[12], (N_B_LAYERS, D_MODEL))
    b_w_in = nrm(ks[13], (N_B_LAYERS, D_MODEL, 2 * DIFF_WIDTH), D_MODEL ** -0.5)
    b_lambda = nrm(ks[14], (N_B_LAYERS, 4, DIFF_HEAD_DIM), 0.1)
    b_sub_g = gain(ks[15], (N_B_LAYERS, DIFF_V_DIM))
    b_w_out = nrm(ks[16], (N_B_LAYERS, DIFF_WIDTH, D_MODEL), DIFF_WIDTH ** -0.5)
    final_norm_g = gain(ks[17], (D_MODEL,))
    return {"x": x, "a_norm_g": a_norm_g, "a_w_in": a_w_in, "a_conv_w": a_conv_w,
            "a_conv_b": a_conv_b, "a_dt_bias": a_dt_bias, "a_a_log": a_a_log,
            "a_d_skip": a_d_skip, "a_gate_norm_g": a_gate_norm_g, "a_w_out": a_w_out,
            "kv_norm_g": kv_norm_g, "w_kv": w_kv, "b_norm_g": b_norm_g, "b_w_in": b_w_in,
            "b_lambda": b_lambda, "b_sub_g": b_sub_g, "b_w_out": b_w_out,
            "final_norm_g": final_norm_g}


def reference(x, a_norm_g, a_w_in, a_conv_w, a_conv_b, a_dt_bias, a_a_log, a_d_skip,
              a_gate_norm_g, a_w_out, kv_norm_g, w_kv, b_norm_g, b_w_in, b_lambda,
              b_sub_g, b_w_out, final_norm_g):
    k = v = None
    for i in range(DEPTH):
        if i < N_A_LAYERS:
            x = mamba2_layer(x, a_norm_g[i], a_w_in[i], a_conv_w[i], a_conv_b[i],
                             a_dt_bias[i], a_a_log[i], a_d_skip[i], a_gate_norm_g[i], a_w_out[i])
            if i == N_A_LAYERS - 1:
                k, v = shared_kv(x, kv_norm_g, w_kv)
        else:
            j = i - N_A_LAYERS
            lambda_init = 0.8 - 0.6 * math.exp(-0.3 * i)
            x = diff_attention_layer(x, k, v, b_norm_g[j], b_w_in[j], b_lambda[j],
                                     b_sub_g[j], b_w_out[j], lambda_init)
    return rms_norm(x, final_norm_g)
```

```python
import contextlib
import math
import numpy as np
import ml_dtypes
import concourse.bass as bass
import concourse.mybir as mybir
from concourse.bass_utils import run_bass_kernel_spmd

F32 = mybir.dt.float32
BF16 = mybir.dt.bfloat16
AF = mybir.ActivationFunctionType
ALU = mybir.AluOpType

S = 4096
DM = 1024
NCH = 64
EPS = 1e-5
NEG = -30000.0


class _Rec:
    def __init__(self):
        self.call = None

    def __getattr__(self, name):
        def f(*a, **k):
            assert self.call is None
            self.call = (name, a, k)
            return None
        return f


def _snap(fn):
    r = _Rec()
    fn(r)
    name, a, k = r.call
    return lambda eng: getattr(eng, name)(*a, **k)


class Sched:
    ENG = ("pe", "act", "dve", "pool", "sp")

    def __init__(self):
        self.ops = {e: [] for e in self.ENG}
        self.lastw = {}
        self.readers = {}
        self.slots = {}
        self.slot_order = []

    def _collect(self, eng, reads, writes):
        deps = []
        for r in reads:
            t = self.lastw.get(r)
            if t is not None:
                deps.append(t)
        for w in writes:
            t = self.lastw.get(w)
            if t is not None and not (t[0] == "c" and t[1] == eng):
                deps.append(t)
            for t in self.readers.get(w, ()):
                if t[0] == "c" and t[1] == eng:
                    continue
                deps.append(t)
        if eng == "pe":
            deps = [t for t in deps if not (t[0] == "c" and t[1] == "pe")]
        return deps

    def _commit(self, tok, reads, writes):
        for w in writes:
            self.lastw[w] = tok
            self.readers[w] = []
        for r in reads:
            if r in writes:
                continue
            self.readers.setdefault(r, []).append(tok)

    def barrier(self):
        toks = []
        for e in self.ENG:
            if self.ops[e]:
                for i in range(len(self.ops[e]) - 1, -1, -1):
                    if self.ops[e][i][2] is None:
                        toks.append(("c", e, i))
                        break
        for s_, c in self.slots.items():
            toks.append(("d", s_, c))
        self.pending_barrier = {e: list(toks) for e in self.ENG}

    def _bar(self, eng, deps):
        pb = getattr(self, "pending_barrier", None)
        if pb and pb.get(eng):
            deps = deps + [t for t in pb[eng] if not (t[0] == "c" and t[1] == eng)]
            pb[eng] = None
        return deps

    def op(self, eng, fn, reads=(), writes=()):
        deps = self._bar(eng, self._collect(eng, reads, writes))
        idx = len(self.ops[eng])
        self.ops[eng].append([deps, _snap(fn), None])
        tok = ("c", eng, idx)
        self._commit(tok, reads, writes)
        return tok

    def dma(self, eng, fn, slot, reads=(), writes=()):
        deps = self._bar(eng, self._collect(eng, reads, writes))
        if slot not in self.slots:
            self.slots[slot] = 0
            self.slot_order.append(slot)
        self.slots[slot] += 16
        tok = ("d", slot, self.slots[slot])
        self.ops[eng].append([deps, _snap(fn), slot])
        self._commit(tok, reads, writes)
        return tok

    def finalize(self):
        need = {e: set() for e in self.ENG}
        for e in self.ENG:
            for deps, fn, t in self.ops[e]:
                for d in deps:
                    if d[0] == "c":
                        need[d[1]].add(d[2])
        self.sigval = {}
        for e in self.ENG:
            c = 0
            for i, o in enumerate(self.ops[e]):
                if o[2] is None and i in need[e]:
                    c += 1
                    self.sigval[(e, i)] = c
        self.need = need

    def emit(self, nc, block, sems, dsems, final_slots):
        engobj = {"pe": "tensor", "act": "scalar", "dve": "vector", "pool": "gpsimd", "sp": "sync"}
        for e in self.ENG:
            ops = self.ops[e]
            need = self.need[e]

            def body(eng, e=e, ops=ops, need=need):
                waited = {}
                for i, (deps, fn, slot) in enumerate(ops):
                    w = {}
                    for d in deps:
                        if d[0] == "c":
                            key = ("c", d[1]); val = self.sigval[(d[1], d[2])]
                        else:
                            key = ("d", d[1]); val = d[2]
                        if waited.get(key, 0) >= val:
                            continue
                        w[key] = max(w.get(key, 0), val)
                    for key, val in w.items():
                        waited[key] = val
                        sem = sems[key[1]] if key[0] == "c" else dsems[key[1]]
                        eng.wait_ge(sem, val)
                    ins = fn(eng)
                    if slot is not None:
                        ins.then_inc(dsems[slot], 16)
                    elif i in need:
                        ins.then_inc(sems[e], 1)
                if e == "sp":
                    for s_ in final_slots:
                        eng.wait_ge(dsems[s_], self.slots[s_])
            getattr(block, engobj[e])(body)


class Prog:
    def __init__(self, dbg=None, nlayers=4):
        self.nc = bass.Bass("TRN2", target_bir_lowering=False)
        self.S = Sched()
        self.es = contextlib.ExitStack()
        self.dbg = dbg or {}
        self.uid = 0

    def dram_in(self, name, shape, dt=F32):
        return self.nc.dram_tensor(name, list(shape), dt, kind="ExternalInput").ap()

    def dram_out(self, name, shape, dt=F32):
        return self.nc.dram_tensor(name, list(shape), dt, kind="ExternalOutput").ap()

    def dram(self, name, shape, dt):
        return self.nc.dram_tensor(name, list(shape), dt, kind="Internal").ap()

    def sb(self, name, shape, dt):
        return self.es.enter_context(self.nc.sbuf_tensor(name, list(shape), dt))

    def ps(self, name, shape, dt=F32):
        return self.es.enter_context(self.nc.psum_tensor(name, list(shape), dt))


class Arena:
    def __init__(self, t, nelem):
        self.t = t
        self.n = nelem
        self.off = 0

    def reset(self, off=0):
        self.off = off

    def alloc(self, shape, dt):
        free = 1
        for d in shape[1:]:
            free *= d
        nb = free * (2 if dt == F32 else 1)
        nb = (nb + 15) // 16 * 16
        assert self.off + nb <= self.n, ("arena overflow", self.off, nb, self.n)
        ap = self.t[0:shape[0], self.off:self.off + nb]
        self.off += nb
        if dt == F32:
            ap = ap.bitcast(F32)
        if free != ap.shape[1]:
            ap = ap[:, 0:free]
        if len(shape) == 3:
            ap = ap.rearrange("p (a b) -> p a b", b=shape[2])
        elif len(shape) == 4:
            ap = ap.rearrange("p (a b c) -> p a b c", b=shape[2], c=shape[3])
        return ap


def bc_mid(ap, n):
    sh = ap.shape
    return ap.unsqueeze(1).to_broadcast([sh[0], n, sh[1]])


def bc_last(ap, n):
    sh = ap.shape
    return ap.unsqueeze(2).to_broadcast([sh[0], sh[1], n])


def build_program(n_mamba=2, n_attn=2, do_final=True, dbg_out=None, stop_after=None):
    P = Prog()
    nc, Sc = P.nc, P.S
    es = P.es
    with es:
        xT_in = P.dram_in("xT", [DM, S])
        a_w_in = P.dram_in("a_w_in", [2, DM, 6176])
        a_w_out = P.dram_in("a_w_out", [2, 2048, DM])
        a_norm_g = P.dram_in("a_norm_g", [128, 2, 8])
        a_conv_w = P.dram_in("a_conv_w", [128, 2, 32, 4])
        a_conv_b = P.dram_in("a_conv_b", [128, 2, 32])
        a_hp = P.dram_in("a_hp", [128, 2, 2])
        a_dskip = P.dram_in("a_dskip", [128, 2, 32])
        a_gng = P.dram_in("a_gng", [128, 2, 16])
        kv_norm_g = P.dram_in("kv_norm_g", [128, 8])
        w_kv = P.dram_in("w_kv", [DM, 2048])
        b_norm_g = P.dram_in("b_norm_g", [128, 2, 8])
        b_w_in = P.dram_in("b_w_in", [2, DM, 2048])
        b_w_out = P.dram_in("b_w_out", [2, DM, DM])
        b_lam = P.dram_in("b_lam", [128, 2, 4, 64])
        b_subg = P.dram_in("b_subg", [128, 2])
        final_g = P.dram_in("final_g", [128, 8])
        c_kaug = P.dram_in("c_kaug", [2, S], BF16)
        c_qaug = P.dram_in("c_qaug", [8, 2, S], BF16)
        c_abias = P.dram_in("c_abias", [128, 8, 35])
        c_mdiag = P.dram_in("c_mdiag", [128, 8, 128])
        c_ident = P.dram_in("c_ident", [128, 128])
        c_ind = P.dram_in("c_ind", [128, 32])
        c_negm = P.dram_in("c_negm", [128, 64])
        c_segm = P.dram_in("c_segm", [128, S], BF16)
        c_onid = P.dram_in("c_onid", [128, 64])
        outT = P.dram_out("outT", [DM, S])

        xr = P.dram("xr", [DM, S], F32)
        zs = P.dram("zs", [2048, S], BF16)
        Xtok = P.dram("Xtok", [S, 2048], BF16)
        Btok = P.dram("Btok", [S, 1024], BF16)
        BTd = P.dram("BTd", [1024, S], BF16)
        CTd = P.dram("CTd", [1024, S], BF16)
        yTd = P.dram("yTd", [2048, S], BF16)
        KTd = P.dram("KTd", [8, 2, 66, S], BF16)
        QTd = P.dram("QTd", [8, 2, 66, S], BF16)
        Vtok = P.dram("Vtok", [S, DM], BF16)
        GTd = P.dram("GTd", [DM, S], BF16)
        YTd = P.dram("YTd", [DM, S], BF16)

        ident_f = P.sb("ident_f", [128, 128], F32)
        ident_b = P.sb("ident_b", [128, 128], BF16)
        ones_b = P.sb("ones_b", [128, 128], BF16)
        ones_f = P.sb("ones_f", [128, 128], F32)
        ind_f = P.sb("ind_f", [128, 32], F32)
        indblk = P.sb("indblk", [128, 32, 64], BF16)
        negm_f = P.sb("negm_f", [128, 64], F32)
        onid_f = P.sb("onid_f", [128, 64], F32)
        onid_b = P.sb("onid_b", [128, 64], BF16)
        epsb = P.sb("epsb", [128, 1], F32)
        oneb = P.sb("oneb", [128, 1], F32)
        normg = P.sb("normg", [128, 2, 8], F32)
        convw = P.sb("convw", [128, 2, 32, 4], F32)
        convb = P.sb("convb", [128, 2, 32], F32)
        hp = P.sb("hp", [128, 2, 2], F32)
        dskip = P.sb("dskip", [128, 2, 32], F32)
        gng = P.sb("gng", [128, 2, 16], F32)
        kvg = P.sb("kvg", [128, 8], F32)
        bng = P.sb("bng", [128, 2, 8], F32)
        fing = P.sb("fing", [128, 8], F32)
        blam = P.sb("blam", [128, 2, 4, 64], F32)
        bsubg = P.sb("bsubg", [128, 2], F32)
        lamt = P.sb("lamt", [128, 8], F32)
        AL = P.sb("AL", [128, S], BF16)
        TMtok = P.sb("TMtok", [64, NCH, 32], BF16)
        Ebc = P.sb("Ebc", [128, NCH, 32], F32)
        ea = P.sb("ea", [128, 1], F32)
        ARENA_N = 84544
        arena_t = P.sb("arena", [128, ARENA_N], BF16)
        AR = Arena(arena_t, ARENA_N)

        def ld(dst, src, slot, writes):
            Sc.dma("sp", lambda e: e.dma_start(out=dst, in_=src), slot, writes=writes)

        ld(ident_f[:], c_ident[:, :], "c0", ["ident_f"])
        ld(ind_f[:], c_ind[:, :], "c1", ["ind_f"])
        ld(negm_f[:], c_negm[:, :], "c2", ["negm_f"])
        ld(normg[:], a_norm_g[:, :, :], "c3", ["normg"])
        ld(convw[:], a_conv_w[:, :, :, :], "c4", ["convw"])
        ld(convb[:], a_conv_b[:, :, :], "c5", ["convb"])
        ld(hp[:], a_hp[:, :, :], "c6", ["hp"])
        ld(dskip[:], a_dskip[:, :, :], "c7", ["dskip"])
        ld(gng[:], a_gng[:, :, :], "c8", ["gng"])
        ld(onid_f[:], c_onid[:, :], "c9", ["onid_f"])
        ld(kvg[:], kv_norm_g[:, :], "c10", ["normg"])
        ld(bng[:], b_norm_g[:, :, :], "c11", ["normg"])
        ld(fing[:], final_g[:, :], "c12", ["normg"])
        ld(blam[:], b_lam[:, :, :, :], "c13", ["blam"])
        ld(bsubg[:], b_subg[:, :], "c14", ["bsubg"])
        for h in range(8):
            for i in range(2):
                Sc.dma("sp", lambda e, h=h, i=i: e.dma_start(out=KTd[h, i, 64:66, :], in_=c_kaug[:, :]), "caug", writes=[("dram", "KTd")])
                Sc.dma("sp", lambda e, h=h, i=i: e.dma_start(out=QTd[h, i, 64:66, :], in_=c_qaug[h, :, :]), "caug", writes=[("dram", "QTd")])
        Sc.op("pool", lambda e: e.memset(ones_b[:], 1.0), writes=["ones_b"])
        Sc.op("pool", lambda e: e.memset(ones_f[:], 1.0), writes=["ones_f"])
        Sc.op("pool", lambda e: e.memset(epsb[:], EPS), writes=["epsb"])
        Sc.op("pool", lambda e: e.memset(oneb[:], 1.0), writes=["oneb"])
        Sc.op("dve", lambda e: e.tensor_copy(out=ident_b[:], in_=ident_f[:]), reads=["ident_f"], writes=["ident_b"])
        Sc.op("dve", lambda e: e.tensor_copy(out=onid_b[:], in_=onid_f[:]), reads=["onid_f"], writes=["onid_b"])
        Sc.op("dve", lambda e: e.tensor_copy(out=indblk[:], in_=bc_last(ind_f[:], 64)), reads=["ind_f"], writes=["indblk"])

        psA = P.ps("psA", [128, 2048], F32)
        psB = P.ps("psB", [128, 2048], F32)

        def rms_stage(src, g_ap_fn, hT, bufs):
            xt_buf, sq_buf, rs_buf = bufs
            srcv = src.rearrange("(c p) t -> p c t", p=128)
            for tt in range(8):
                xb = xt_buf[tt % 2]
                xname = "xt%d" % (tt % 2)
                tsl = slice(tt * 512, (tt + 1) * 512)
                Sc.dma("sp", lambda e, xb=xb, tsl=tsl: e.dma_start(out=xb, in_=srcv[:, :, tsl]),
                       "ld_" + xname, reads=[("dram", src.name, 2 * tt), ("dram", src.name, 2 * tt + 1)], writes=[xname])
                Sc.op("act", lambda e, xb=xb: e.activation(out=sq_buf, in_=xb, func=AF.Square),
                      reads=[xname], writes=["sq"])
                for c in range(8):
                    Sc.op("pe", lambda e, c=c: e.matmul(psA[:, 0:512], lhsT=ones_b[:], rhs=sq_buf[:, c, :],
                                                       start=(c == 0), stop=(c == 7)),
                          reads=["sq", "ones_b"], writes=["psA0"])
                rs = rs_buf[tt % 2]
                rname = "rs%d" % (tt % 2)
                Sc.op("act", lambda e, rs=rs: e.activation(out=rs, in_=psA[:, 0:512], func=AF.Ln,
                                                            scale=1.0 / DM, bias=epsb[:]),
                      reads=["psA0", "epsb"], writes=[rname])
                Sc.op("act", lambda e, rs=rs: e.activation(out=rs, in_=rs, func=AF.Exp, scale=-0.5),
                      reads=[rname], writes=[rname])
                for c in range(8):
                    Sc.op("dve", lambda e, c=c, xb=xb, rs=rs, tsl=tsl: e.scalar_tensor_tensor(
                        out=hT[:, c, tsl], in0=xb[:, c, :], scalar=g_ap_fn(c), in1=rs,
                        op0=ALU.mult, op1=ALU.mult),
                        reads=[xname, rname, "normg"], writes=[("hT", tt)])

        src_x = xT_in
        for L in range(n_mamba):
            Sc.barrier()
            AR.reset()
            hT = AR.alloc([128, 8, S], BF16)
            offB = AR.off
            xt_buf = [AR.alloc([128, 8, 512], F32) for i in range(2)]
            sq_buf = AR.alloc([128, 8, 512], BF16)
            rs_buf = [AR.alloc([128, 512], F32) for i in range(2)]
            rms_stage(src_x, lambda c, L=L: normg[:, L, c:c + 1], hT, (xt_buf, sq_buf, rs_buf))

            Sc.barrier()
            AR.reset(offB)
            wst = [AR.alloc([128, 8, 128], F32) for i in range(2)]
            wbf = [AR.alloc([128, 8, 128], BF16) for i in range(2)]
            U = [AR.alloc([128, 3 + S + 13], BF16)]
            XO = [AR.alloc([128, S], BF16) for i in range(2)]
            XTk = [AR.alloc([128, 32, 128], BF16)]
            dg = [AR.alloc([128, 4, 128], BF16)]
            bufA = AR.alloc([128, S], F32)
            bufB = AR.alloc([128, S], F32)
            bufC = AR.alloc([128, S], F32)
            hiB = AR.alloc([128, S], BF16)
            segm = hiB
            Eblk = XO[1][0:32, :].bitcast(F32).rearrange("p (a b) -> p a b", b=32)
            Sc.op("pool", lambda e: e.memset(U[0][:, 0:3], 0.0), writes=["U0"])
            Sc.dma("sp", lambda e: e.dma_start(out=segm, in_=c_segm[:, :]), "ld_segm", writes=["hiB"])

            win = a_w_in[L]
            winv = win.rearrange("(c p) n -> p c n", p=128)
            order = [("dt", 0)] + [("B", g) for g in range(8)] + [("C", g) for g in range(8)] + \
                    [("x", j) for j in range(16)] + [("z", j) for j in range(16)]
            def prepB(ci):
                kind, j = order[ci]
                wb = ci % 2
                ws, wbt = wst[wb], wbf[wb]
                wsn, wbn = "wst%d" % wb, "wbf%d" % wb
                if kind == "dt":
                    for rep in range(4):
                        Sc.dma("sp", lambda e, ws=ws, rep=rep: e.dma_start(
                            out=ws[:, :, rep * 32:(rep + 1) * 32], in_=winv[:, :, 6144:6176]),
                            "ld_" + wsn, writes=[wsn])
                else:
                    col0 = {"z": 0, "x": 2048, "B": 4096, "C": 5120}[kind] + j * 128
                    Sc.dma("sp", lambda e, ws=ws, col0=col0: e.dma_start(out=ws, in_=winv[:, :, col0:col0 + 128]),
                           "ld_" + wsn, writes=[wsn])
                Sc.op("dve", lambda e, ws=ws, wbt=wbt: e.tensor_copy(out=wbt, in_=ws), reads=[wsn], writes=[wbn])
                if kind in ("x", "B", "C"):
                    cch = {"x": 0, "B": 16, "C": 24}[kind] + j
                    for tap in range(4):
                        Sc.op("dve", lambda e, tap=tap, cch=cch: e.tensor_scalar(
                            out=dg[0][:, tap, :], in0=ident_f[:], scalar1=convw[:, L, cch, tap:tap + 1], scalar2=None,
                            op0=ALU.mult), reads=["ident_f", "convw"], writes=["dg0"])

            prepB(0)
            for ci, (kind, j) in enumerate(order):
                wb = ci % 2
                ws, wbt = wst[wb], wbf[wb]
                wsn, wbn = "wst%d" % wb, "wbf%d" % wb
                ub = ci % 2
                Ub, XOb = U[0], XO[ub]
                Un, XOn = "U0", "XO%d" % ub
                conv = kind in ("x", "B", "C")
                if conv:
                    cch = {"x": 0, "B": 16, "C": 24}[kind] + j
                    dgb = dg[0]
                    dgn = "dg0"
                for tt in range(8):
                    bank = tt % 4
                    pst = psA[:, bank * 512:(bank + 1) * 512]
                    pn = "psA%d" % bank
                    for c in range(8):
                        Sc.op("pe", lambda e, pst=pst, c=c, tt=tt, wbt=wbt: e.matmul(
                            pst, lhsT=wbt[:, c, :], rhs=hT[:, c, tt * 512:(tt + 1) * 512], start=(c == 0), stop=(c == 7)),
                            reads=[wbn, ("hT", tt)], writes=[pn])
                    tsl = slice(tt * 512, (tt + 1) * 512)
                    if kind == "z":
                        Sc.op("act", lambda e, pst=pst, tsl=tsl, XOb=XOb: e.activation(out=XOb[:, tsl], in_=pst, func=AF.Silu),
                              reads=[pn], writes=[XOn])
                    elif kind == "dt":
                        Sc.op("act", lambda e, pst=pst, tsl=tsl: e.activation(out=bufA[:, tsl], in_=pst, func=AF.Exp,
                                                                                 bias=hp[:, L, 0:1], scale=1.0),
                              reads=[pn, "hp"], writes=["bufA"])
                    else:
                        Sc.op("dve", lambda e, pst=pst, tt=tt, Ub=Ub: e.tensor_copy(out=Ub[:, 3 + tt * 512:3 + (tt + 1) * 512], in_=pst),
                              reads=[pn], writes=[Un])
                if conv:
                    for tt in range(8):
                        bank = tt % 4
                        pst = psB[:, bank * 512:(bank + 1) * 512]
                        pn = "psB%d" % bank
                        for tap in range(4):
                            Sc.op("pe", lambda e, pst=pst, tap=tap, tt=tt, Ub=Ub, dgb=dgb: e.matmul(
                                pst, lhsT=dgb[:, tap, :], rhs=Ub[:, tt * 512 + tap:tt * 512 + tap + 512],
                                start=(tap == 0), stop=(tap == 3)), reads=[dgn, Un], writes=[pn])
                        tsl = slice(tt * 512, (tt + 1) * 512)
                        Sc.op("act", lambda e, pst=pst, tsl=tsl, XOb=XOb, cch=cch: e.activation(
                            out=XOb[:, tsl], in_=pst, func=AF.Silu, bias=convb[:, L, cch:cch + 1], scale=1.0),
                            reads=[pn, "convb"], writes=[XOn])
                if ci + 1 < len(order):
                    prepB(ci + 1)
                if kind == "z":
                    Sc.dma("pool", lambda e, XOb=XOb, j=j: e.dma_start(out=zs[j * 128:(j + 1) * 128, :], in_=XOb),
                           "st_" + XOn, reads=[XOn], writes=[("dram", "zs")])
                elif kind in ("B", "C"):
                    dst = BTd if kind == "B" else CTd
                    Sc.dma("pool", lambda e, XOb=XOb, j=j, dst=dst: e.dma_start(out=dst[j * 128:(j + 1) * 128, :], in_=XOb),
                           "st_" + XOn, reads=[XOn], writes=[("dram", dst.name)])
                if kind in ("x", "B"):
                    XTb = XTk[0]
                    XTn = "XTk0"
                    for q in range(4):
                        bank = q % 2
                        pbt = psB[:, bank * 512:bank * 512 + 512].bitcast(BF16)
                        pn = "psB%d" % bank
                        for i8 in range(8):
                            tb = q * 8 + i8
                            Sc.op("pe", lambda e, pbt=pbt, i8=i8, tb=tb, XOb=XOb: e.transpose(
                                pbt[:, i8 * 128:(i8 + 1) * 128], XOb[:, tb * 128:(tb + 1) * 128], ident_b[:]),
                                reads=[XOn, "ident_b"], writes=[pn])
                        Sc.op("dve", lambda e, pbt=pbt, q=q, XTb=XTb: e.tensor_copy(
                            out=XTb[:, q * 8:(q + 1) * 8, :], in_=pbt.rearrange("p (a b) -> p a b", b=128)),
                            reads=[pn], writes=[XTn])
                    dst = Xtok if kind == "x" else Btok
                    dstv = dst.rearrange("(tb p) c -> p tb c", p=128)
                    Sc.dma("pool", lambda e, XTb=XTb, dstv=dstv, j=j: e.dma_start(
                        out=dstv[:, :, j * 128:(j + 1) * 128], in_=XTb),
                        "st_" + XTn, reads=[XTn], writes=[("dram", dst.name)])
                if kind == "dt":
                    Sc.op("act", lambda e: e.activation(out=bufA, in_=bufA, func=AF.Ln, bias=oneb[:], scale=1.0),
                          reads=["bufA", "oneb"], writes=["bufA"])
                    Sc.op("act", lambda e: e.activation(out=ea[:], in_=hp[:, L, 1:2], func=AF.Exp),
                          reads=["hp"], writes=["ea"])
                    Sc.op("dve", lambda e: e.tensor_scalar(out=bufB, in0=bufA, scalar1=ea[:, 0:1], scalar2=-1.0,
                                                           op0=ALU.mult, op1=ALU.mult),
                          reads=["bufA", "ea"], writes=["bufB"])
                    Sc.op("dve", lambda e: e.tensor_tensor_scan(out=bufC, data0=segm, data1=bufB, initial=0.0,
                                                                op0=ALU.mult, op1=ALU.add),
                          reads=["hiB", "bufB"], writes=["bufC"])
                    Sc.op("act", lambda e: e.activation(out=bufB[64:128, :], in_=bufA[64:128, :], func=AF.Ln),
                          reads=["bufA", "bufB"], writes=["bufB"])
                    Sc.op("dve", lambda e: e.tensor_tensor(out=bufC[64:128, :], in0=bufC[64:128, :], in1=bufB[64:128, :], op=ALU.subtract),
                          reads=["bufC", "bufB"], writes=["bufC"])
                    Sc.op("dve", lambda e: e.tensor_copy(out=hiB, in_=bufC), reads=["bufC"], writes=["hiB"])
                    Sc.op("dve", lambda e: e.tensor_copy(out=AL[0:32, :], in_=hiB[0:32, :]), reads=["hiB"], writes=["AL"])
                    Sc.op("dve", lambda e: e.tensor_tensor(out=AL[32:64, :], in0=bufC[32:64, :], in1=hiB[32:64, :], op=ALU.subtract),
                          reads=["bufC", "hiB"], writes=["AL"])
                    Sc.op("dve", lambda e: e.tensor_scalar(out=AL[64:96, :], in0=hiB[64:96, :], scalar1=-1.0, scalar2=None, op0=ALU.mult),
                          reads=["hiB"], writes=["AL"])
                    Sc.op("dve", lambda e: e.tensor_tensor(out=AL[96:128, :], in0=hiB[96:128, :], in1=bufC[96:128, :], op=ALU.subtract),
                          reads=["bufC", "hiB"], writes=["AL"])
                    Sc.op("act", lambda e: e.activation(out=bufA[0:32, :], in_=bufC[0:32, :], func=AF.Exp),
                          reads=["bufC", "bufB"], writes=["bufA"])
                    for c in range(NCH):
                        pst = psB[0:64, (c % 4) * 512:(c % 4) * 512 + 32]
                        pn = "psB%d" % (c % 4)
                        Sc.op("pe", lambda e, pst=pst, c=c: e.transpose(pst, bufA[0:32, c * 64:(c + 1) * 64], ident_f[0:32, 0:32]),
                              reads=["bufA", "ident_f"], writes=[pn])
                        Sc.op("dve", lambda e, pst=pst, c=c: e.tensor_copy(out=TMtok[:, c, :], in_=pst), reads=[pn], writes=["TMtok"])
                    a0 = bufC[0:32, :].rearrange("p (c l) -> p c l", l=64)
                    Sc.op("act", lambda e: e.activation(out=bufB[0:32, 0:NCH], in_=a0[:, :, 63], func=AF.Exp),
                          reads=["bufC", "bufB"], writes=["bufB"])
                    Sc.op("dve", lambda e: e.tensor_tensor(out=Eblk, in0=bc_last(bufB[0:32, 0:NCH], 32),
                                                           in1=bc_mid(ind_f[0:32, :], NCH), op=ALU.mult),
                          reads=["bufB", "ind_f"], writes=["Eblk"])
                    for q in range(4):
                        pst = psB[:, q * 512:(q + 1) * 512]
                        pn = "psB%d" % q
                        Sc.op("pe", lambda e, pst=pst, q=q: e.matmul(
                            pst, lhsT=ones_f[0:32, :], rhs=Eblk[:, q * 16:(q + 1) * 16, :].rearrange("p a b -> p (a b)"),
                            start=True, stop=True), reads=["Eblk", "ones_f"], writes=[pn])
                        Sc.op("act", lambda e, pst=pst, q=q: e.activation(
                            out=Ebc[:, q * 16:(q + 1) * 16, :].rearrange("p a b -> p (a b)"), in_=pst, func=AF.Copy),
                            reads=[pn], writes=["Ebc"])
            if stop_after == ("B", L):
                break

            Sc.barrier()
            AR.reset()
            St = AR.alloc([128, 2048], F32)
            Sbf = AR.alloc([128, 2048], BF16)
            Xq = [AR.alloc([64, 4, 2048], BF16) for i in range(2)]
            Bq = [AR.alloc([64, 4, 1024], BF16) for i in range(2)]
            BTq = [AR.alloc([128, 8, 256], BF16) for i in range(2)]
            CTq = [AR.alloc([128, 8, 256], BF16) for i in range(2)]
            RH = [AR.alloc([128, 32, 64], BF16) for i in range(2)]
            Dc = AR.alloc([64, 32, 64], BF16)
            MT = AR.alloc([64, 32, 64], BF16)
            Xw = AR.alloc([64, 32, 64], BF16)
            XD = [AR.alloc([64, 32, 64], BF16) for i in range(2)]
            t1 = AR.alloc([64, 32, 64], BF16)
            te32 = AR.alloc([64, 32], BF16)
            y3a = AR.alloc([64, 32, 64], BF16)
            y3 = AR.alloc([64, 2048], BF16)
            yTq = [AR.alloc([128, 16, 256], BF16) for i in range(2)]
            Sc.op("pool", lambda e: e.memset(St, 0.0), writes=["St0", "St1", "St2", "St3"])
            Sc.op("pool", lambda e: e.memset(Sbf, 0.0), writes=["Sbf0", "Sbf1", "Sbf2", "Sbf3"])
            for k in range(2):
                Sc.op("dve", lambda e, k=k: e.tensor_copy(out=RH[k][64:128, :, :], in_=bc_mid(negm_f[64:128, :], 32)),
                      reads=["negm_f"], writes=["RH%d" % k])
            Xtv = Xtok.rearrange("(q j p) c -> q p j c", j=4, p=64)
            Btv = Btok.rearrange("(q j p) c -> q p j c", j=4, p=64)
            BTv = BTd.rearrange("(g n) t -> n g t", n=128)
            CTv = CTd.rearrange("(g n) t -> n g t", n=128)
            yTv = yTd.rearrange("(c p) t -> p c t", p=128)

            def loadC(q):
                qb = q % 2
                Xn, Bn, BTn, CTn = "Xq%d" % qb, "Bq%d" % qb, "BTq%d" % qb, "CTq%d" % qb
                Sc.dma("sp", lambda e, q=q, qb=qb: e.dma_start(out=Xq[qb], in_=Xtv[q]), "ld_" + Xn,
                       reads=[("dram", "Xtok")], writes=[Xn])
                Sc.dma("sp", lambda e, q=q, qb=qb: e.dma_start(out=Bq[qb], in_=Btv[q]), "ld_" + Bn,
                       reads=[("dram", "Btok")], writes=[Bn])
                Sc.dma("sp", lambda e, q=q, qb=qb: e.dma_start(out=BTq[qb], in_=BTv[:, :, q * 256:(q + 1) * 256]), "ld_" + BTn,
                       reads=[("dram", "BTd")], writes=[BTn])
                Sc.dma("sp", lambda e, q=q, qb=qb: e.dma_start(out=CTq[qb], in_=CTv[:, :, q * 256:(q + 1) * 256]), "ld_" + CTn,
                       reads=[("dram", "CTd")], writes=[CTn])

            def prepC(c):
                q, jq = c // 4, c % 4
                qb = q % 2
                if jq == 0:
                    loadC(q)
                Xc = Xq[qb][:, jq, :].rearrange("p (h d) -> p h d", d=64)
                tsl = slice(c * 64, (c + 1) * 64)
                rb = c % 2
                Sc.op("pool", lambda e, rb=rb, tsl=tsl: e.tensor_tensor(
                    out=RH[rb][0:64, :, :], in0=bc_mid(AL[0:64, tsl], 32), in1=indblk[0:64, :, :], op=ALU.mult),
                    reads=["AL", "indblk"], writes=["RH%d" % rb])
                Sc.op("pool", lambda e, Xc=Xc, rb=rb: e.tensor_tensor(out=XD[rb], in0=Xc, in1=bc_last(dskip[0:64, L, :], 64), op=ALU.mult),
                      reads=["Xq%d" % qb, "dskip"], writes=["XD%d" % rb])

            def emitD(c):
                tsl = slice(c * 64, (c + 1) * 64)
                RHb, RHn = RH[c % 2], "RH%d" % (c % 2)
                for k4 in range(4):
                    pst = psA[0:64, k4 * 512:(k4 + 1) * 512]
                    pn = "psA%d" % k4
                    rsl = slice(k4 * 8, (k4 + 1) * 8)
                    Sc.op("pe", lambda e, pst=pst, RHb=RHb, rsl=rsl: e.matmul(
                        pst, lhsT=onid_b[:, :], rhs=RHb[:, rsl, :].rearrange("p a b -> p (a b)"), start=True, stop=False),
                        reads=[RHn, "onid_b"], writes=[pn])
                    Sc.op("pe", lambda e, pst=pst, rsl=rsl, tsl=tsl: e.matmul(
                        pst, lhsT=AL[64:128, tsl], rhs=indblk[64:128, rsl, :].rearrange("p a b -> p (a b)"), start=False, stop=True),
                        reads=["AL", "indblk"], writes=[pn])
                Sc.op("act", lambda e: e.activation(out=Dc.rearrange("p a b -> p (a b)"), in_=psA[0:64, :], func=AF.Exp),
                      reads=["psA0", "psA1", "psA2", "psA3"], writes=["Dc"])

            prepC(0)
            emitD(0)
            for c in range(NCH):
                q, jq = c // 4, c % 4
                qb = q % 2
                Xn, Bn, BTn, CTn = "Xq%d" % qb, "Bq%d" % qb, "BTq%d" % qb, "CTq%d" % qb
                Xc = Xq[qb][:, jq, :].rearrange("p (h d) -> p h d", d=64)
                Bc = Bq[qb][:, jq, :]
                csl = slice(jq * 64, (jq + 1) * 64)
                tsl = slice(c * 64, (c + 1) * 64)
                rb = c % 2
                RHb, RHn = RH[rb], "RH%d" % rb
                XDb, XDn = XD[rb], "XD%d" % rb
                for g in range(8):
                    Sc.op("pe", lambda e, g=g, qb=qb, csl=csl: e.matmul(
                        psB[0:64, g * 64:(g + 1) * 64], lhsT=BTq[qb][:, g, csl], rhs=CTq[qb][:, g, csl], start=True, stop=True),
                        reads=[BTn, CTn], writes=["psB0"])
                if c + 1 < NCH:
                    prepC(c + 1)
                cbv = psB[0:64, 0:512].rearrange("p (g l) -> p g l", l=64).unsqueeze(2).to_broadcast([64, 8, 4, 64])
                Sc.op("dve", lambda e, cbv=cbv: e.tensor_tensor(
                    out=MT.rearrange("p (g r) l -> p g r l", r=4), in0=Dc.rearrange("p (g r) l -> p g r l", r=4), in1=cbv, op=ALU.mult),
                    reads=["Dc", "psB0"], writes=["MT"])
                Sc.op("dve", lambda e: e.tensor_copy(out=te32, in_=Dc[:, :, 63]), reads=["Dc"], writes=["te32"])
                for h in range(32):
                    pst = psA[0:64, h * 64:(h + 1) * 64]
                    pn = "psA%d" % (h // 8)
                    Sc.op("pe", lambda e, pst=pst, h=h, Xc=Xc: e.matmul(pst, lhsT=MT[:, h, :], rhs=Xc[:, h, :], start=True, stop=True),
                          reads=["MT", Xn], writes=[pn])
                for g in range(8):
                    pst = psB[0:64, g * 256:(g + 1) * 256]
                    pn = "psB%d" % (g // 2)
                    Sc.op("pe", lambda e, pst=pst, g=g, qb=qb, csl=csl: e.matmul(
                        pst, lhsT=CTq[qb][:, g, csl], rhs=Sbf[:, g * 256:(g + 1) * 256], start=True, stop=True),
                        reads=[CTn, "Sbf%d" % (g // 2)], writes=[pn])
                for k4 in range(4):
                    Sc.op("dve", lambda e, XDb=XDb, k4=k4: e.tensor_tensor(
                        out=y3a[:, k4 * 8:(k4 + 1) * 8, :], in0=psA[0:64, k4 * 512:(k4 + 1) * 512].rearrange("p (h d) -> p h d", d=64),
                        in1=XDb[:, k4 * 8:(k4 + 1) * 8, :], op=ALU.add),
                        reads=["psA%d" % k4, XDn], writes=["y3a%d" % k4])
                if c + 1 < NCH:
                    emitD(c + 1)
                Sc.op("dve", lambda e, c=c: e.tensor_tensor(out=t1, in0=psB[0:64, :].rearrange("p (h d) -> p h d", d=64),
                                                           in1=bc_last(TMtok[:, c, :], 64), op=ALU.mult),
                      reads=["psB0", "psB1", "psB2", "psB3", "TMtok"], writes=["t1"])
                Sc.op("dve", lambda e: e.tensor_tensor(out=y3, in0=y3a.rearrange("p a b -> p (a b)"), in1=t1.rearrange("p a b -> p (a b)"), op=ALU.add),
                      reads=["y3a0", "y3a1", "y3a2", "y3a3", "t1"], writes=["y3"])
                Sc.op("dve", lambda e, Xc=Xc: e.tensor_tensor(out=Xw, in0=Xc, in1=bc_last(te32, 64), op=ALU.mult),
                      reads=[Xn, "te32"], writes=["Xw"])
                pbt = psB[:, 0:512].bitcast(BF16)
                for cc in range(16):
                    Sc.op("pe", lambda e, cc=cc: e.transpose(pbt[:, cc * 64:(cc + 1) * 64], y3[:, cc * 128:(cc + 1) * 128], ident_b[0:64, 0:64]),
                          reads=["y3", "ident_b"], writes=["psB0"])
                yb = yTq[qb]
                yn = "yTq%d" % qb
                Sc.op("act", lambda e, yb=yb, csl=csl: e.activation(out=yb[:, :, csl], in_=pbt.rearrange("p (a b) -> p a b", b=64), func=AF.Copy),
                      reads=["psB0"], writes=[yn])
                if jq == 3:
                    Sc.dma("pool", lambda e, yb=yb, q=q: e.dma_start(out=yTv[:, :, q * 256:(q + 1) * 256], in_=yb), "st_" + yn,
                           reads=[yn], writes=[("dram", "yTd")])
                for g in range(8):
                    pst = psB[:, g * 256:(g + 1) * 256]
                    pn = "psB%d" % (g // 2)
                    Sc.op("pe", lambda e, pst=pst, g=g, Bc=Bc: e.matmul(
                        pst, lhsT=Bc[:, g * 128:(g + 1) * 128], rhs=Xw.rearrange("p a b -> p (a b)")[:, g * 256:(g + 1) * 256],
                        start=True, stop=True), reads=[Bn, "Xw"], writes=[pn])
                Sc.op("pool", lambda e, c=c: e.tensor_tensor(
                    out=St.rearrange("p (h d) -> p h d", d=64), in0=St.rearrange("p (h d) -> p h d", d=64),
                    in1=bc_last(Ebc[:, c, :], 64), op=ALU.mult), reads=["St0", "St1", "St2", "St3"] + ["Ebc"], writes=["St0", "St1", "St2", "St3"])
                for k4 in range(4):
                    Sc.op("dve", lambda e, k4=k4: e.tensor_tensor(out=St[:, k4 * 512:(k4 + 1) * 512], in0=St[:, k4 * 512:(k4 + 1) * 512],
                                                               in1=psB[:, k4 * 512:(k4 + 1) * 512], op=ALU.add),
                          reads=["St%d" % k4, "psB%d" % k4], writes=["St%d" % k4])
                for k4 in range(4):
                    Sc.op("act", lambda e, k4=k4: e.activation(out=Sbf[:, k4 * 512:(k4 + 1) * 512], in_=St[:, k4 * 512:(k4 + 1) * 512], func=AF.Copy),
                          reads=["St%d" % k4], writes=["Sbf%d" % k4])
            if stop_after == ("C", L):
                break

            Sc.barrier()
            AR.reset()
            wo = AR.alloc([128, 16, DM], BF16)
            wos = [AR.alloc([128, 2, DM], F32)]
            yt = [AR.alloc([128, 16, 256], BF16) for i in range(2)]
            zt = [AR.alloc([128, 16, 256], BF16) for i in range(2)]
            ut = [AR.alloc([128, 16, 256], BF16) for i in range(2)]
            usq = [AR.alloc([128, 16, 256], BF16) for i in range(2)]
            rsd = AR.alloc([128, 8, 256], F32)
            un = [AR.alloc([128, 16, 256], BF16) for i in range(2)]
            xres = [AR.alloc([128, 8, 256], F32) for i in range(2)]
            wov = a_w_out[L].rearrange("(c p) d -> p c d", p=128)
            for k8 in range(8):
                wsb = wos[0]
                wsn = "wos0"
                Sc.dma("sp", lambda e, wsb=wsb, k8=k8: e.dma_start(out=wsb, in_=wov[:, k8 * 2:(k8 + 1) * 2, :]), "ld_" + wsn, writes=[wsn])
                Sc.op("dve" if k8 % 2 == 0 else "act", lambda e, wsb=wsb, k8=k8: (
                    e.tensor_copy(out=wo[:, k8 * 2:(k8 + 1) * 2, :], in_=wsb) if k8 % 2 == 0 else
                    e.activation(out=wo[:, k8 * 2:(k8 + 1) * 2, :], in_=wsb, func=AF.Copy)), reads=[wsn], writes=["wo"])
            zsv = zs.rearrange("(c p) t -> p c t", p=128)
            srcv = src_x.rearrange("(c p) t -> p c t", p=128)
            xrv = xr.rearrange("(c p) t -> p c t", p=128)

            def frontD(tt):
                b2 = tt % 2
                tsl = slice(tt * 256, (tt + 1) * 256)
                Sc.dma("sp", lambda e, b2=b2, tsl=tsl: e.dma_start(out=yt[b2], in_=yTv[:, :, tsl]), "ld_yt%d" % b2,
                       reads=[("dram", "yTd")], writes=["yt%d" % b2])
                Sc.dma("sp", lambda e, b2=b2, tsl=tsl: e.dma_start(out=zt[b2], in_=zsv[:, :, tsl]), "ld_zt%d" % b2,
                       reads=[("dram", "zs")], writes=["zt%d" % b2])
                Sc.dma("sp", lambda e, b2=b2, tsl=tsl: e.dma_start(out=xres[b2], in_=srcv[:, :, tsl]), "ld_xres%d" % b2,
                       reads=[("dram", src_x.name, tt)], writes=["xres%d" % b2])
                Sc.op("dve", lambda e, b2=b2: e.tensor_tensor(out=ut[b2], in0=yt[b2], in1=zt[b2], op=ALU.mult),
                      reads=["yt%d" % b2, "zt%d" % b2], writes=["ut%d" % b2])
                Sc.op("act", lambda e, b2=b2: e.activation(out=usq[b2], in_=ut[b2], func=AF.Square), reads=["ut%d" % b2], writes=["usq%d" % b2])

            def midD(tt):
                b2 = tt % 2
                for g in range(8):
                    pst = psA[:, g * 256:(g + 1) * 256]
                    pn = "psA%d" % (g // 2)
                    for k2 in range(2):
                        Sc.op("pe", lambda e, pst=pst, g=g, k2=k2, b2=b2: e.matmul(pst, lhsT=ones_b[:], rhs=usq[b2][:, 2 * g + k2, :],
                                                                                    start=(k2 == 0), stop=(k2 == 1)),
                              reads=["usq%d" % b2, "ones_b"], writes=[pn])
                Sc.op("act", lambda e: e.activation(out=rsd.rearrange("p a b -> p (a b)"), in_=psA[:, :], func=AF.Ln,
                                                    scale=1.0 / 256.0, bias=epsb[:]),
                      reads=["psA0", "psA1", "psA2", "psA3", "epsb"], writes=["rsd"])
                Sc.op("act", lambda e: e.activation(out=rsd, in_=rsd, func=AF.Exp, scale=-0.5), reads=["rsd"], writes=["rsd"])
                for cc in range(16):
                    Sc.op("dve", lambda e, cc=cc, b2=b2: e.scalar_tensor_tensor(
                        out=un[b2][:, cc, :], in0=ut[b2][:, cc, :], scalar=gng[:, L, cc:cc + 1], in1=rsd[:, cc // 2, :],
                        op0=ALU.mult, op1=ALU.mult), reads=["ut%d" % b2, "rsd", "gng"], writes=["un%d" % b2])

            def backD(tt):
                b2 = tt % 2
                tsl = slice(tt * 256, (tt + 1) * 256)
                xrb, xrn = xres[b2], "xres%d" % b2
                for dc in range(8):
                    pst = psB[:, (dc % 4) * 512:(dc % 4) * 512 + 256]
                    pn = "psB%d" % (dc % 4)
                    for cc in range(16):
                        Sc.op("pe", lambda e, pst=pst, dc=dc, cc=cc, b2=b2: e.matmul(
                            pst, lhsT=wo[:, cc, dc * 128:(dc + 1) * 128], rhs=un[b2][:, cc, :], start=(cc == 0), stop=(cc == 15)),
                            reads=["wo", "un%d" % b2], writes=[pn])
                    Sc.op("dve", lambda e, pst=pst, dc=dc, xrb=xrb: e.tensor_tensor(out=xrb[:, dc, :], in0=xrb[:, dc, :], in1=pst, op=ALU.add),
                          reads=[pn, xrn], writes=[xrn])
                Sc.dma("pool", lambda e, xrb=xrb, tsl=tsl: e.dma_start(out=xrv[:, :, tsl], in_=xrb), "st_" + xrn,
                       reads=[xrn], writes=[("dram", "xr", tt)])

            frontD(0)
            midD(0)
            frontD(1)
            for tt in range(16):
                if tt + 1 < 16:
                    midD(tt + 1)
                backD(tt)
                if tt + 2 < 16:
                    frontD(tt + 2)
            src_x = xr


        def proj_stage(hT, wv, specs, offB):
            Sc.barrier()
            AR.reset(offB)
            wst = [AR.alloc([128, 8, 128], F32) for i in range(2)]
            wbf = [AR.alloc([128, 8, 128], BF16) for i in range(2)]
            XO = [AR.alloc([128, S], BF16) for i in range(2)]
            XTk = [AR.alloc([128, 32, 128], BF16)]
            def prepP(ci):
                col0 = specs[ci][0]
                wb = ci % 2
                ws, wbt = wst[wb], wbf[wb]
                wsn, wbn = "wst%d" % wb, "wbf%d" % wb
                Sc.dma("sp", lambda e, ws=ws, col0=col0: e.dma_start(out=ws, in_=wv[:, :, col0:col0 + 128]),
                       "ld_" + wsn, writes=[wsn])
                Sc.op("dve", lambda e, ws=ws, wbt=wbt: e.tensor_copy(out=wbt, in_=ws), reads=[wsn], writes=[wbn])

            prepP(0)
            for ci, (col0, evac, dst) in enumerate(specs):
                wb = ci % 2
                ws, wbt = wst[wb], wbf[wb]
                wsn, wbn = "wst%d" % wb, "wbf%d" % wb
                XOb, XOn = XO[ci % 2], "XO%d" % (ci % 2)
                for tt in range(8):
                    bank = tt % 4
                    pst = psA[:, bank * 512:(bank + 1) * 512]
                    pn = "psA%d" % bank
                    for c in range(8):
                        Sc.op("pe", lambda e, pst=pst, c=c, tt=tt, wbt=wbt: e.matmul(
                            pst, lhsT=wbt[:, c, :], rhs=hT[:, c, tt * 512:(tt + 1) * 512], start=(c == 0), stop=(c == 7)),
                            reads=[wbn, ("hT", tt)], writes=[pn])
                    tsl = slice(tt * 512, (tt + 1) * 512)
                    if evac == "silu":
                        Sc.op("act", lambda e, pst=pst, tsl=tsl, XOb=XOb: e.activation(out=XOb[:, tsl], in_=pst, func=AF.Silu),
                              reads=[pn], writes=[XOn])
                    elif evac == "q":
                        Sc.op("act", lambda e, pst=pst, tsl=tsl, XOb=XOb: e.activation(out=XOb[:, tsl], in_=pst, func=AF.Copy, scale=0.125),
                              reads=[pn], writes=[XOn])
                    else:
                        Sc.op("dve", lambda e, pst=pst, tsl=tsl, XOb=XOb: e.tensor_copy(out=XOb[:, tsl], in_=pst),
                              reads=[pn], writes=[XOn])
                if ci + 1 < len(specs):
                    prepP(ci + 1)
                kind, dt_, j = dst
                if kind == "feat":
                    Sc.dma("pool", lambda e, XOb=XOb, j=j, dt_=dt_: e.dma_start(out=dt_[j * 128:(j + 1) * 128, :], in_=XOb),
                           "st_" + XOn, reads=[XOn], writes=[("dram", dt_.name)])
                elif kind == "head2":
                    for i in range(2):
                        Sc.dma("pool", lambda e, XOb=XOb, j=j, dt_=dt_, i=i: e.dma_start(
                            out=dt_[j, i, 0:64, :], in_=XOb[i * 64:(i + 1) * 64, :]),
                            "st_" + XOn, reads=[XOn], writes=[("dram", dt_.name)])
                else:
                    XTb, XTn = XTk[0], "XTk0"
                    for q in range(4):
                        bank = q % 2
                        pbt = psB[:, bank * 512:bank * 512 + 512].bitcast(BF16)
                        pn = "psB%d" % bank
                        for i8 in range(8):
                            tb = q * 8 + i8
                            Sc.op("pe", lambda e, pbt=pbt, i8=i8, tb=tb, XOb=XOb: e.transpose(
                                pbt[:, i8 * 128:(i8 + 1) * 128], XOb[:, tb * 128:(tb + 1) * 128], ident_b[:]),
                                reads=[XOn, "ident_b"], writes=[pn])
                        Sc.op("dve", lambda e, pbt=pbt, q=q, XTb=XTb: e.tensor_copy(
                            out=XTb[:, q * 8:(q + 1) * 8, :], in_=pbt.rearrange("p (a b) -> p a b", b=128)),
                            reads=[pn], writes=[XTn])
                    dstv = dt_.rearrange("(tb p) c -> p tb c", p=128)
                    Sc.dma("pool", lambda e, XTb=XTb, dstv=dstv, j=j: e.dma_start(
                        out=dstv[:, :, j * 128:(j + 1) * 128], in_=XTb),
                        "st_" + XTn, reads=[XTn], writes=[("dram", dt_.name)])

        def norm_alloc():
            Sc.barrier()
            AR.reset()
            hT = AR.alloc([128, 8, S], BF16)
            offB = AR.off
            xt_buf = [AR.alloc([128, 8, 512], F32) for i in range(2)]
            sq_buf = AR.alloc([128, 8, 512], BF16)
            rs_buf = [AR.alloc([128, 512], F32) for i in range(2)]
            return hT, offB, (xt_buf, sq_buf, rs_buf)

        if n_attn > 0:
            hT, offB, bufs = norm_alloc()
            rms_stage(src_x, lambda c: kvg[:, c:c + 1], hT, bufs)
            wkvv = w_kv.rearrange("(c p) n -> p c n", p=128)
            specs = [(h * 128, "copy", ("head2", KTd, h)) for h in range(8)] + \
                    [(1024 + h * 128, "copy", ("tok", Vtok, h)) for h in range(8)]
            proj_stage(hT, wkvv, specs, offB)

        fused_state = None
        final_done = False
        for j in range(n_attn):
            li = 2 + j
            lam_init = 0.8 - 0.6 * math.exp(-0.3 * li)
            if fused_state is not None:
                hT, offB = fused_state
            else:
                hT, offB, bufs = norm_alloc()
                rms_stage(src_x, lambda c, j=j: bng[:, j, c:c + 1], hT, bufs)
            wqv = b_w_in[j].rearrange("(c p) n -> p c n", p=128)
            specs = [(h * 128, "q", ("head2", QTd, h)) for h in range(8)] + \
                    [(1024 + h * 128, "silu", ("feat", GTd, h)) for h in range(8)]
            proj_stage(hT, wqv, specs, offB)

            Sc.barrier()
            AR.reset()
            Kh = [[AR.alloc([66, S], BF16) for i in range(2)] for b in range(2)]
            Qh = [[AR.alloc([66, S], BF16) for i in range(2)] for b in range(2)]
            Vh = [AR.alloc([128, 32, 128], BF16) for b in range(2)]
            Pt = [AR.alloc([128, 2, 512], BF16) for b in range(2)]
            abias = AR.alloc([128, 8, 35], F32)
            mdiag_f = AR.alloc([128, 8, 128], F32)
            mdiag = AR.alloc([128, 8, 128], BF16)
            r12 = AR.alloc([128, 2, 512], F32)
            o12 = AR.alloc([128, 2, 512], F32)
            ocm = AR.alloc([128, 512], F32)
            osq = AR.alloc([128, 512], BF16)
            rst = AR.alloc([128, 512], F32)
            gt = [AR.alloc([128, 512], BF16) for b in range(3)]
            yo = [AR.alloc([128, 512], BF16) for b in range(2)]
            lt = AR.alloc([128, 2, 64], F32)
            Sc.dma("sp", lambda e: e.dma_start(out=abias, in_=c_abias[:, :, :]), "ld_abias", writes=["abias"])
            Sc.dma("sp", lambda e: e.dma_start(out=mdiag_f, in_=c_mdiag[:, :, :]), "ld_mdiag", writes=["mdiag_f"])
            Sc.op("dve", lambda e: e.tensor_copy(out=mdiag, in_=mdiag_f), reads=["mdiag_f"], writes=["mdiag"])
            for i in range(2):
                Sc.op("dve", lambda e, i=i: e.tensor_tensor(out=lt[:, i, :], in0=blam[:, j, 2 * i, :], in1=blam[:, j, 2 * i + 1, :], op=ALU.mult),
                      reads=["blam"], writes=["lt"])
                Sc.op("dve", lambda e, i=i: e.reduce_sum(out=lamt[:, i:i + 1], in_=lt[:, i, :], axis=mybir.AxisListType.X),
                      reads=["lt"], writes=["lamt"])
            Sc.op("act", lambda e: e.activation(out=lamt[:, 2:4], in_=lamt[:, 0:2], func=AF.Exp), reads=["lamt"], writes=["lamt"])
            Sc.op("dve", lambda e: e.scalar_tensor_tensor(out=lamt[:, 4:5], in0=lamt[:, 3:4], scalar=-lam_init, in1=lamt[:, 2:3],
                                                          op0=ALU.add, op1=ALU.subtract), reads=["lamt"], writes=["lamt"])
            Sc.op("dve", lambda e: e.tensor_scalar(out=lamt[:, 5:6], in0=bsubg[:, j:j + 1], scalar1=1.0 - lam_init, scalar2=None, op0=ALU.mult),
                  reads=["bsubg", "lamt"], writes=["lamt"])
            psS = [psA[:, 0:1024].rearrange("p (a b) -> p a b", b=512), psA[:, 1024:2048].rearrange("p (a b) -> p a b", b=512)]
            psSn = [["psA0", "psA1"], ["psA2", "psA3"]]
            psO = psB[:, 0:1024].rearrange("p (a b) -> p a b", b=512)
            psZ = psB[:, 1024:2048].rearrange("p (a b) -> p a b", b=512)
            Vtv = Vtok.rearrange("(tb p) c -> p tb c", p=128)
            def load_head(h):
                hb = h % 2
                Kn, Qn, Vn = "Kh%d" % hb, "Qh%d" % hb, "Vh%d" % hb
                for i in range(2):
                    Sc.dma("sp", lambda e, h=h, i=i, hb=hb: e.dma_start(out=Kh[hb][i], in_=KTd[h, i, :, :]), "ld_" + Kn,
                           reads=[("dram", "KTd")], writes=[Kn])
                    Sc.dma("sp", lambda e, h=h, i=i, hb=hb: e.dma_start(out=Qh[hb][i], in_=QTd[h, i, :, :]), "ld_" + Qn,
                           reads=[("dram", "QTd")], writes=[Qn])
                Sc.dma("sp", lambda e, h=h, hb=hb: e.dma_start(out=Vh[hb], in_=Vtv[:, :, h * 128:(h + 1) * 128]), "ld_" + Vn,
                       reads=[("dram", "Vtok")], writes=[Vn])

            seq = [(h, g, kb, i) for h in range(8) for g in range(8) for kb in range(4 * g + 4) for i in range(2)]
            NB = 4
            LA = 3
            psS1 = [psA[:, b * 512:(b + 1) * 512] for b in range(NB)]
            Pt1 = [Pt[b // 2][:, b % 2, :] for b in range(NB)]
            st = {"buf": {}}

            def emit_qk(idx):
                h, g, kb, i = seq[idx]
                hb = h % 2
                Kn, Qn = "Kh%d" % hb, "Qh%d" % hb
                q0 = g * 512
                if g == 0 and kb == 0 and h == 0 and i == 0:
                    load_head(0)
                if kb == 0 and i == 0:
                    eb = (h * 8 + g) % 3
                    Sc.dma("sp", lambda e, h=h, q0=q0, eb=eb: e.dma_start(out=gt[eb], in_=GTd[h * 128:(h + 1) * 128, q0:q0 + 512]),
                           "ld_gt%d" % eb, reads=[("dram", "GTd")], writes=["gt%d" % eb])
                n0 = max(0, kb - 4 * g) * 128
                sb_ = idx % NB
                st["buf"][idx] = sb_
                Sc.op("pe", lambda e, sb_=sb_, i=i, hb=hb, kb=kb, n0=n0, q0=q0: e.matmul(
                    psS1[sb_][:, n0:512], lhsT=Kh[hb][i][:, kb * 128:(kb + 1) * 128], rhs=Qh[hb][i][:, q0 + n0:q0 + 512],
                    start=True, stop=True, skip_group_check=True), reads=[Kn, Qn], writes=["psA%d" % sb_])
                if kb - 4 * g >= 0:
                    Sc.op("pe", lambda e, sb_=sb_, n0=n0, h=h: e.matmul(
                        psS1[sb_][:, n0:n0 + 128], lhsT=ident_b[:], rhs=mdiag[:, h, :],
                        start=False, stop=True, skip_group_check=True), reads=["ident_b", "mdiag"], writes=["psA%d" % sb_])

            def emit_exp_pv(idx):
                h, g, kb, i = seq[idx]
                hb = h % 2
                Vn = "Vh%d" % hb
                nkb = 4 * g + 4
                dj = kb - 4 * g
                n0 = max(0, dj) * 128
                sb_ = st["buf"][idx]
                pS, pSn = psS1[sb_], "psA%d" % sb_
                Pb, Pn = Pt1[sb_], "Pt%d" % sb_
                Sc.op("act", lambda e, pS=pS, Pb=Pb, n0=n0, h=h, dj=dj: e.activation(
                    out=Pb[:, n0:512], in_=pS[:, n0:512], func=AF.Exp, bias=abias[:, h, dj + 31:dj + 32], scale=1.0),
                    reads=[pSn, "abias"], writes=[Pn])
                Sc.op("pe", lambda e, Pb=Pb, i=i, hb=hb, kb=kb, n0=n0, nkb=nkb: e.matmul(
                    psO[:, i, n0:512], lhsT=Vh[hb][:, kb, :], rhs=Pb[:, n0:512],
                    start=(kb == 0), stop=(kb == nkb - 1), skip_group_check=True),
                    reads=[Vn, Pn], writes=["psB%d" % i])
                Sc.op("pe", lambda e, Pb=Pb, i=i, kb=kb, n0=n0, nkb=nkb: e.matmul(
                    psZ[:, i, n0:512], lhsT=ones_b[:], rhs=Pb[:, n0:512],
                    start=(kb == 0), stop=(kb == nkb - 1), skip_group_check=True),
                    reads=["ones_b", Pn], writes=["psB%d" % (2 + i)])

            def emit_E1(h, g):
                for i in range(2):
                    Sc.op("dve", lambda e, i=i: e.tensor_copy(out=r12[:, i, :], in_=psZ[:, i, :]), reads=["psB%d" % (2 + i)], writes=["r12"])
                    Sc.op("act", lambda e, i=i: e.activation(out=o12[:, i, :], in_=psO[:, i, :], func=AF.Copy), reads=["psB%d" % i], writes=["o12"])
                Sc.op("dve", lambda e: e.reciprocal(out=r12, in_=r12), reads=["r12"], writes=["r12"])
                Sc.op("dve", lambda e: e.tensor_tensor(out=o12, in0=o12, in1=r12, op=ALU.mult),
                      reads=["o12", "r12"], writes=["o12"])
                Sc.op("dve", lambda e: e.scalar_tensor_tensor(out=ocm, in0=o12[:, 1, :], scalar=lamt[:, 4:5], in1=o12[:, 0, :],
                                                              op0=ALU.mult, op1=ALU.add), reads=["o12", "lamt"], writes=["ocm"])
                Sc.op("dve", lambda e: e.tensor_tensor(out=osq, in0=ocm, in1=ocm, op=ALU.mult), reads=["ocm"], writes=["osq"])

            def emit_E2(h, g, sb_):
                q0 = g * 512
                eb = (h * 8 + g) % 3
                pS, pSn = psS1[sb_], "psA%d" % sb_
                Sc.op("pe", lambda e, pS=pS: e.matmul(pS, lhsT=ones_b[:], rhs=osq, start=True, stop=True),
                      reads=["ones_b", "osq"], writes=[pSn])
                Sc.op("act", lambda e, pS=pS: e.activation(out=rst, in_=pS, func=AF.Ln, scale=1.0 / 128.0, bias=epsb[:]),
                      reads=[pSn, "epsb"], writes=["rst"])
                Sc.op("act", lambda e: e.activation(out=rst, in_=rst, func=AF.Exp, scale=-0.5), reads=["rst"], writes=["rst"])
                Sc.op("dve", lambda e: e.scalar_tensor_tensor(out=ocm, in0=ocm, scalar=lamt[:, 5:6], in1=rst, op0=ALU.mult, op1=ALU.mult),
                      reads=["ocm", "lamt", "rst"], writes=["ocm"])
                yb, yn = yo[eb % 2], "yo%d" % (eb % 2)
                Sc.op("dve", lambda e, yb=yb, eb=eb: e.tensor_tensor(out=yb, in0=ocm, in1=gt[eb], op=ALU.mult),
                      reads=["ocm", "gt%d" % eb], writes=[yn])
                Sc.dma("pool", lambda e, yb=yb, h=h, q0=q0: e.dma_start(out=YTd[h * 128:(h + 1) * 128, q0:q0 + 512], in_=yb),
                       "st_" + yn, reads=[yn], writes=[("dram", "YTd")])

            pending = None
            E2_DELAY = 24
            for t in range(LA):
                emit_qk(t)
            for idx in range(len(seq)):
                h, g, kb, i = seq[idx]
                if idx + LA < len(seq):
                    emit_qk(idx + LA)
                emit_exp_pv(idx)
                if g == 0 and kb == 0 and i == 0 and h + 1 < 8:
                    load_head(h + 1)
                if pending is not None and (2 * kb + i) == min(E2_DELAY, 2 * (4 * g + 4) - 1):
                    emit_E2(pending[0], pending[1], st["buf"][idx])
                    pending = None
                if kb == 4 * g + 3 and i == 1:
                    emit_E1(h, g)
                    pending = (h, g)
            emit_E2(pending[0], pending[1], 0)

            Sc.barrier()
            AR.reset()
            last = (j == n_attn - 1)
            fuse_next = (not last)
            fuse_final = (last and do_final)
            hT_next = AR.alloc([128, 8, S], BF16) if fuse_next else None
            offB_next = AR.off
            wo = AR.alloc([128, 8, DM], BF16)
            wos = [AR.alloc([128, 2, DM], F32) for i in range(2)]
            ytl = [AR.alloc([128, 8, 512], BF16) for i in range(2)]
            xres = [AR.alloc([128, 8, 512], F32) for i in range(2)]
            sqp = AR.alloc([128, 8, 512], BF16)
            rsp = [AR.alloc([128, 512], F32) for i in range(2)]
            fo = [AR.alloc([128, 8, 512], F32) for i in range(2)] if fuse_final else None
            outv = outT.rearrange("(c p) t -> p c t", p=128)
            wov = b_w_out[j].rearrange("(c p) d -> p c d", p=128)
            for k4 in range(4):
                wsb = wos[k4 % 2]
                wsn = "wos%d" % (k4 % 2)
                Sc.dma("sp", lambda e, wsb=wsb, k4=k4: e.dma_start(out=wsb, in_=wov[:, k4 * 2:(k4 + 1) * 2, :]), "ld_" + wsn, writes=[wsn])
                Sc.op("dve" if k4 % 2 == 0 else "pool", lambda e, wsb=wsb, k4=k4: e.tensor_copy(out=wo[:, k4 * 2:(k4 + 1) * 2, :], in_=wsb), reads=[wsn], writes=["wo"])
            YTv = YTd.rearrange("(c p) t -> p c t", p=128)
            srcv = src_x.rearrange("(c p) t -> p c t", p=128)
            xrv = xr.rearrange("(c p) t -> p c t", p=128)
            for tt in range(8):
                b2 = tt % 2
                tsl = slice(tt * 512, (tt + 1) * 512)
                ytb, xrb = ytl[b2], xres[b2]
                ytn, xrn = "ytl%d" % b2, "xres%d" % b2
                Sc.dma("sp", lambda e, ytb=ytb, tsl=tsl: e.dma_start(out=ytb, in_=YTv[:, :, tsl]), "ld_" + ytn,
                       reads=[("dram", "YTd")], writes=[ytn])
                Sc.dma("sp", lambda e, xrb=xrb, tsl=tsl: e.dma_start(out=xrb, in_=srcv[:, :, tsl]), "ld_" + xrn,
                       reads=[("dram", src_x.name, 2 * tt), ("dram", src_x.name, 2 * tt + 1)], writes=[xrn])
                for dc in range(8):
                    pst = psA[:, (dc % 4) * 512:(dc % 4 + 1) * 512]
                    pn = "psA%d" % (dc % 4)
                    for cc in range(8):
                        Sc.op("pe", lambda e, pst=pst, dc=dc, cc=cc, ytb=ytb: e.matmul(
                            pst, lhsT=wo[:, cc, dc * 128:(dc + 1) * 128], rhs=ytb[:, cc, :], start=(cc == 0), stop=(cc == 7)),
                            reads=["wo", ytn], writes=[pn])
                    Sc.op("dve", lambda e, pst=pst, dc=dc, xrb=xrb: e.tensor_tensor(out=xrb[:, dc, :], in0=xrb[:, dc, :], in1=pst, op=ALU.add),
                          reads=[pn, xrn], writes=[xrn])
                if not fuse_final:
                    Sc.dma("pool", lambda e, xrb=xrb, tsl=tsl: e.dma_start(out=xrv[:, :, tsl], in_=xrb), "st_" + xrn,
                           reads=[xrn], writes=[("dram", "xr", 2 * tt), ("dram", "xr", 2 * tt + 1)])
                if fuse_next or fuse_final:
                    Sc.op("act", lambda e, xrb=xrb: e.activation(out=sqp, in_=xrb, func=AF.Square), reads=[xrn], writes=["sqp"])
                    for c in range(8):
                        Sc.op("pe", lambda e, c=c: e.matmul(psB[:, 0:512], lhsT=ones_b[:], rhs=sqp[:, c, :], start=(c == 0), stop=(c == 7)),
                              reads=["sqp", "ones_b"], writes=["psB0"])
                    rs, rname = rsp[b2], "rsp%d" % b2
                    Sc.op("act", lambda e, rs=rs: e.activation(out=rs, in_=psB[:, 0:512], func=AF.Ln, scale=1.0 / DM, bias=epsb[:]),
                          reads=["psB0", "epsb"], writes=[rname])
                    Sc.op("act", lambda e, rs=rs: e.activation(out=rs, in_=rs, func=AF.Exp, scale=-0.5), reads=[rname], writes=[rname])
                    for c in range(8):
                        if fuse_next:
                            Sc.op("dve", lambda e, c=c, xrb=xrb, rs=rs, tsl=tsl: e.scalar_tensor_tensor(
                                out=hT_next[:, c, tsl], in0=xrb[:, c, :], scalar=bng[:, j + 1, c:c + 1], in1=rs, op0=ALU.mult, op1=ALU.mult),
                                reads=[xrn, rname, "normg"], writes=[("hT", tt)])
                        else:
                            Sc.op("dve", lambda e, c=c, xrb=xrb, rs=rs: e.scalar_tensor_tensor(
                                out=fo[b2][:, c, :], in0=xrb[:, c, :], scalar=fing[:, c:c + 1], in1=rs, op0=ALU.mult, op1=ALU.mult),
                                reads=[xrn, rname, "normg"], writes=["fo%d" % b2])
                    if fuse_final:
                        Sc.dma("pool", lambda e, tsl=tsl, b2=b2: e.dma_start(out=outv[:, :, tsl], in_=fo[b2]), "st_out",
                               reads=["fo%d" % b2], writes=[("dram", "outT")])
            src_x = xr
            fused_state = (hT_next, offB_next) if fuse_next else None
            if fuse_final:
                final_done = True

        final_slots = []
        if final_done:
            final_slots.append("st_out")
        if do_final and not final_done:
            Sc.barrier()
            AR.reset()
            xt_buf = [AR.alloc([128, 8, 512], F32) for i in range(2)]
            sq_buf = AR.alloc([128, 8, 512], BF16)
            rs_buf = [AR.alloc([128, 512], F32) for i in range(2)]
            fo = [AR.alloc([128, 8, 512], F32) for i in range(2)]
            srcv = src_x.rearrange("(c p) t -> p c t", p=128)
            outv = outT.rearrange("(c p) t -> p c t", p=128)
            for tt in range(8):
                xb = xt_buf[tt % 2]
                xname = "xt%d" % (tt % 2)
                tsl = slice(tt * 512, (tt + 1) * 512)
                Sc.dma("sp", lambda e, xb=xb, tsl=tsl: e.dma_start(out=xb, in_=srcv[:, :, tsl]),
                       "ld_" + xname, reads=[("dram", src_x.name, 2 * tt), ("dram", src_x.name, 2 * tt + 1)], writes=[xname])
                Sc.op("act", lambda e, xb=xb: e.activation(out=sq_buf, in_=xb, func=AF.Square), reads=[xname], writes=["sq"])
                for c in range(8):
                    Sc.op("pe", lambda e, c=c: e.matmul(psA[:, 0:512], lhsT=ones_b[:], rhs=sq_buf[:, c, :], start=(c == 0), stop=(c == 7)),
                          reads=["sq", "ones_b"], writes=["psA0"])
                rs = rs_buf[tt % 2]
                rname = "rs%d" % (tt % 2)
                Sc.op("act", lambda e, rs=rs: e.activation(out=rs, in_=psA[:, 0:512], func=AF.Ln, scale=1.0 / DM, bias=epsb[:]),
                      reads=["psA0", "epsb"], writes=[rname])
                Sc.op("act", lambda e, rs=rs: e.activation(out=rs, in_=rs, func=AF.Exp, scale=-0.5), reads=[rname], writes=[rname])
                fb, fn_ = fo[tt % 2], "fo%d" % (tt % 2)
                for c in range(8):
                    Sc.op("dve", lambda e, c=c, xb=xb, rs=rs, fb=fb: e.scalar_tensor_tensor(
                        out=fb[:, c, :], in0=xb[:, c, :], scalar=fing[:, c:c + 1], in1=rs, op0=ALU.mult, op1=ALU.mult),
                        reads=[xname, rname, "normg"], writes=[fn_])
                Sc.dma("pool", lambda e, fb=fb, tsl=tsl: e.dma_start(out=outv[:, :, tsl], in_=fb), "st_out",
                       reads=[fn_], writes=[("dram", "outT")])
            final_slots.append("st_out")

        if dbg_out:
            Sc.barrier()
            for name, (srcname, shape, dt) in dbg_out.items():
                o = P.dram_out(name, shape, dt)
                src = {"zs": zs, "Xtok": Xtok, "Btok": Btok, "BTd": BTd, "CTd": CTd, "yTd": yTd, "xr": xr,
                       "KTd": KTd, "QTd": QTd, "Vtok": Vtok, "GTd": GTd, "YTd": YTd}.get(srcname)
                if src is not None:
                    Sc.dma("sp", lambda e, o=o, src=src: e.dma_start(out=o, in_=src), "dbg_" + name,
                           reads=[("dram", src.name)] + [("dram", src.name, t) for t in range(16)], writes=[("dram", name)])
                else:
                    sbt = {"TMtok": TMtok, "Ebc": Ebc, "AL": AL}[srcname]
                    Sc.dma("sp", lambda e, o=o, sbt=sbt: e.dma_start(out=o, in_=sbt[:]), "dbg_" + name,
                           reads=[srcname], writes=[("dram", name)])
                final_slots.append("dbg_" + name)

        Sc.finalize()
        sems = {}
        for e in Sc.ENG:
            sems[e] = es.enter_context(nc.semaphore("s_" + e))
        dsems = {}
        for i, s_ in enumerate(Sc.slot_order):
            dsems[s_] = es.enter_context(nc.semaphore("d%d" % i))
        block = es.enter_context(nc.Block())
        Sc.emit(nc, block, sems, dsems, final_slots)
    return nc


def host_constants():
    ident = np.eye(128, dtype=np.float32)
    ind = np.zeros((128, 32), np.float32)
    ind[np.arange(128), np.arange(128) % 32] = 1.0
    s = np.arange(64)[:, None]; l = np.arange(64)[None, :]
    negm = np.zeros((128, 64), np.float32)
    negm[64:] = np.where(l < s, NEG, 0.0)
    segm = np.ones((128, S), np.float32)
    segm[:, ::64] = 0.0
    onid = np.ones((128, 64), np.float32)
    onid[64:] = np.eye(64, dtype=np.float32)
    bf = ml_dtypes.bfloat16
    slopes = 2.0 ** (-8.0 * np.arange(1, 9) / 8.0)
    q = np.arange(S)
    qm = q % 512
    qaug = np.stack([-(slopes[:, None]) * (qm & ~1)[None, :], -(slopes[:, None]) * (qm & 1)[None, :]], axis=1)
    kaug = np.ones((2, S), np.float32)
    kl = np.arange(128)[:, None, None]
    d = (np.arange(35) - 31)[None, None, :]
    abias = slopes[None, :, None] * (kl + 128.0 * d)
    k = np.arange(128)[:, None]; qq = np.arange(128)[None, :]
    kc, qc = k // 64, qq // 64
    md = np.zeros((128, 8, 128), np.float64)
    for h in range(8):
        m = np.where(kc > qc, NEG, np.where((kc == qc) & (k > qq), -2.0 * slopes[h] * (k - qq), 0.0))
        md[:, h, :] = m
    return {"c_ident": ident, "c_ind": ind, "c_negm": negm, "c_segm": segm.astype(bf), "c_onid": onid,
            "c_kaug": kaug.astype(bf), "c_qaug": qaug.astype(np.float32).astype(bf),
            "c_abias": abias.astype(np.float32), "c_mdiag": md.astype(np.float32)}


def host_layout(inp):
    f = np.float32
    d = {}
    d["a_w_in"] = np.ascontiguousarray(inp["a_w_in"], dtype=f)
    d["a_w_out"] = np.ascontiguousarray(inp["a_w_out"], dtype=f)
    d["a_norm_g"] = np.ascontiguousarray(inp["a_norm_g"].reshape(2, 8, 128).transpose(2, 0, 1), dtype=f)
    d["a_conv_w"] = np.ascontiguousarray(inp["a_conv_w"].reshape(2, 4, 32, 128).transpose(3, 0, 2, 1), dtype=f)
    d["a_conv_b"] = np.ascontiguousarray(inp["a_conv_b"].reshape(2, 32, 128).transpose(2, 0, 1), dtype=f)
    hp = np.stack([inp["a_dt_bias"], inp["a_a_log"]], axis=-1)
    d["a_hp"] = np.ascontiguousarray(np.tile(hp.transpose(1, 0, 2), (4, 1, 1)), dtype=f)
    d["a_dskip"] = np.ascontiguousarray(np.broadcast_to(inp["a_d_skip"][None], (128, 2, 32)), dtype=f)
    d["a_gng"] = np.ascontiguousarray(inp["a_gate_norm_g"].reshape(2, 16, 128).transpose(2, 0, 1), dtype=f)
    d["kv_norm_g"] = np.ascontiguousarray(inp["kv_norm_g"].reshape(8, 128).T, dtype=f)
    d["w_kv"] = np.ascontiguousarray(inp["w_kv"], dtype=f)
    d["b_norm_g"] = np.ascontiguousarray(inp["b_norm_g"].reshape(2, 8, 128).transpose(2, 0, 1), dtype=f)
    d["b_w_in"] = np.ascontiguousarray(inp["b_w_in"], dtype=f)
    d["b_w_out"] = np.ascontiguousarray(inp["b_w_out"], dtype=f)
    d["b_lam"] = np.ascontiguousarray(np.broadcast_to(inp["b_lambda"][None], (128, 2, 4, 64)), dtype=f)
    d["b_subg"] = np.ascontiguousarray(inp["b_sub_g"].T, dtype=f)
    d["final_g"] = np.ascontiguousarray(inp["final_norm_g"].reshape(8, 128).T, dtype=f)
    d.update(host_constants())
    return d


_NC_CACHE = {}


def kernel(**inputs):
    inp = {k: np.asarray(v) for k, v in inputs.items()}
    if "nc" not in _NC_CACHE:
        _NC_CACHE["nc"] = build_program()
    nc = _NC_CACHE["nc"]
    hl = host_layout(inp)
    x = inp["x"].astype(np.float32)
    maps = []
    for b in range(8):
        m = dict(hl)
        m["xT"] = np.ascontiguousarray(x[b].T)
        maps.append(m)
    res = run_bass_kernel_spmd(nc, maps, core_ids=list(range(8)))
    out = np.stack([np.ascontiguousarray(res.results[b]["outT"].T) for b in range(8)], axis=0)
    return out.astype(np.float32)
```

```python
import contextlib
import math
import numpy as np
import ml_dtypes
import concourse.bass as bass
import concourse.mybir as mybir
from concourse.bass_utils import run_bass_kernel_spmd

F32 = mybir.dt.float32
BF16 = mybir.dt.bfloat16
AF = mybir.ActivationFunctionType
ALU = mybir.AluOpType

S = 4096
DM = 1024
NCH = 64
EPS = 1e-5
NEG = -30000.0


class _Rec:
    def __init__(self):
        self.call = None

    def __getattr__(self, name):
        def f(*a, **k):
            assert self.call is None
            self.call = (name, a, k)
            return None
        return f


def _snap(fn):
    r = _Rec()
    fn(r)
    name, a, k = r.call
    return lambda eng: getattr(eng, name)(*a, **k)


class Sched:
    ENG = ("pe", "act", "dve", "pool", "sp")

    def __init__(self):
        self.ops = {e: [] for e in self.ENG}
        self.lastw = {}
        self.readers = {}
        self.slots = {}
        self.slot_order = []

    def _collect(self, eng, reads, writes):
        deps = []
        for r in reads:
            t = self.lastw.get(r)
            if t is not None:
                deps.append(t)
        for w in writes:
            t = self.lastw.get(w)
            if t is not None and not (t[0] == "c" and t[1] == eng):
                deps.append(t)
            for t in self.readers.get(w, ()):
                if t[0] == "c" and t[1] == eng:
                    continue
                deps.append(t)
        if eng == "pe":
            deps = [t for t in deps if not (t[0] == "c" and t[1] == "pe")]
        return deps

    def _commit(self, tok, reads, writes):
        for w in writes:
            self.lastw[w] = tok
            self.readers[w] = []
        for r in reads:
            if r in writes:
                continue
            self.readers.setdefault(r, []).append(tok)

    def barrier(self):
        toks = []
        for e in self.ENG:
            if self.ops[e]:
                for i in range(len(self.ops[e]) - 1, -1, -1):
                    if self.ops[e][i][2] is None:
                        toks.append(("c", e, i))
                        break
        for s_, c in self.slots.items():
            toks.append(("d", s_, c))
        self.pending_barrier = {e: list(toks) for e in self.ENG}

    def _bar(self, eng, deps):
        pb = getattr(self, "pending_barrier", None)
        if pb and pb.get(eng):
            deps = deps + [t for t in pb[eng] if not (t[0] == "c" and t[1] == eng)]
            pb[eng] = None
        return deps

    def op(self, eng, fn, reads=(), writes=()):
        deps = self._bar(eng, self._collect(eng, reads, writes))
        idx = len(self.ops[eng])
        self.ops[eng].append([deps, _snap(fn), None])
        tok = ("c", eng, idx)
        self._commit(tok, reads, writes)
        return tok

    def dma(self, eng, fn, slot, reads=(), writes=()):
        deps = self._bar(eng, self._collect(eng, reads, writes))
        if slot not in self.slots:
            self.slots[slot] = 0
            self.slot_order.append(slot)
        self.slots[slot] += 16
        tok = ("d", slot, self.slots[slot])
        self.ops[eng].append([deps, _snap(fn), slot])
        self._commit(tok, reads, writes)
        return tok

    def finalize(self):
        need = {e: set() for e in self.ENG}
        for e in self.ENG:
            for deps, fn, t in self.ops[e]:
                for d in deps:
                    if d[0] == "c":
                        need[d[1]].add(d[2])
        self.sigval = {}
        for e in self.ENG:
            c = 0
            for i, o in enumerate(self.ops[e]):
                if o[2] is None and i in need[e]:
                    c += 1
                    self.sigval[(e, i)] = c
        self.need = need

    def emit(self, nc, block, sems, dsems, final_slots):
        engobj = {"pe": "tensor", "act": "scalar", "dve": "vector", "pool": "gpsimd", "sp": "sync"}
        for e in self.ENG:
            ops = self.ops[e]
            need = self.need[e]

            def body(eng, e=e, ops=ops, need=need):
                waited = {}
                for i, (deps, fn, slot) in enumerate(ops):
                    w = {}
                    for d in deps:
                        if d[0] == "c":
                            key = ("c", d[1]); val = self.sigval[(d[1], d[2])]
                        else:
                            key = ("d", d[1]); val = d[2]
                        if waited.get(key, 0) >= val:
                            continue
                        w[key] = max(w.get(key, 0), val)
                    for key, val in w.items():
                        waited[key] = val
                        sem = sems[key[1]] if key[0] == "c" else dsems[key[1]]
                        eng.wait_ge(sem, val)
                    ins = fn(eng)
                    if slot is not None:
                        ins.then_inc(dsems[slot], 16)
                    elif i in need:
                        ins.then_inc(sems[e], 1)
                if e == "sp":
                    for s_ in final_slots:
                        eng.wait_ge(dsems[s_], self.slots[s_])
            getattr(block, engobj[e])(body)


class Prog:
    def __init__(self, dbg=None, nlayers=4):
        self.nc = bass.Bass("TRN2", target_bir_lowering=False)
        self.S = Sched()
        self.es = contextlib.ExitStack()
        self.dbg = dbg or {}
        self.uid = 0

    def dram_in(self, name, shape, dt=F32):
        return self.nc.dram_tensor(name, list(shape), dt, kind="ExternalInput").ap()

    def dram_out(self, name, shape, dt=F32):
        return self.nc.dram_tensor(name, list(shape), dt, kind="ExternalOutput").ap()

    def dram(self, name, shape, dt):
        return self.nc.dram_tensor(name, list(shape), dt, kind="Internal").ap()

    def sb(self, name, shape, dt):
        return self.es.enter_context(self.nc.sbuf_tensor(name, list(shape), dt))

    def ps(self, name, shape, dt=F32):
        return self.es.enter_context(self.nc.psum_tensor(name, list(shape), dt))


class Arena:
    def __init__(self, t, nelem):
        self.t = t
        self.n = nelem
        self.off = 0

    def reset(self, off=0):
        self.off = off

    def alloc(self, shape, dt):
        free = 1
        for d in shape[1:]:
            free *= d
        nb = free * (2 if dt == F32 else 1)
        nb = (nb + 15) // 16 * 16
        assert self.off + nb <= self.n, ("arena overflow", self.off, nb, self.n)
        ap = self.t[0:shape[0], self.off:self.off + nb]
        self.off += nb
        if dt == F32:
            ap = ap.bitcast(F32)
        if free != ap.shape[1]:
            ap = ap[:, 0:free]
        if len(shape) == 3:
            ap = ap.rearrange("p (a b) -> p a b", b=shape[2])
        elif len(shape) == 4:
            ap = ap.rearrange("p (a b c) -> p a b c", b=shape[2], c=shape[3])
        return ap


def bc_mid(ap, n):
    sh = ap.shape
    return ap.unsqueeze(1).to_broadcast([sh[0], n, sh[1]])


def bc_last(ap, n):
    sh = ap.shape
    return ap.unsqueeze(2).to_broadcast([sh[0], sh[1], n])


def build_program(n_mamba=2, n_attn=2, do_final=True, dbg_out=None, stop_after=None):
    P = Prog()
    nc, Sc = P.nc, P.S
    es = P.es
    with es:
        xT_in = P.dram_in("xT", [DM, S])
        a_w_in = P.dram_in("a_w_in", [2, DM, 6176])
        a_w_out = P.dram_in("a_w_out", [2, 2048, DM])
        a_norm_g = P.dram_in("a_norm_g", [128, 2, 8])
        a_conv_w = P.dram_in("a_conv_w", [128, 2, 32, 4])
        a_conv_b = P.dram_in("a_conv_b", [128, 2, 32])
        a_hp = P.dram_in("a_hp", [128, 2, 2])
        a_dskip = P.dram_in("a_dskip", [128, 2, 32])
        a_gng = P.dram_in("a_gng", [128, 2, 16])
        kv_norm_g = P.dram_in("kv_norm_g", [128, 8])
        w_kv = P.dram_in("w_kv", [DM, 2048])
        b_norm_g = P.dram_in("b_norm_g", [128, 2, 8])
        b_w_in = P.dram_in("b_w_in", [2, DM, 2048])
        b_w_out = P.dram_in("b_w_out", [2, DM, DM])
        b_lam = P.dram_in("b_lam", [128, 2, 4, 64])
        b_subg = P.dram_in("b_subg", [128, 2])
        final_g = P.dram_in("final_g", [128, 8])
        c_kaug = P.dram_in("c_kaug", [2, S], BF16)
        c_qaug = P.dram_in("c_qaug", [8, 2, S], BF16)
        c_abias = P.dram_in("c_abias", [128, 8, 35])
        c_mdiag = P.dram_in("c_mdiag", [128, 8, 128])
        c_ident = P.dram_in("c_ident", [128, 128])
        c_ind = P.dram_in("c_ind", [128, 32])
        c_negm = P.dram_in("c_negm", [128, 64])
        c_segm = P.dram_in("c_segm", [128, S], BF16)
        c_onid = P.dram_in("c_onid", [128, 64])
        outT = P.dram_out("outT", [DM, S])

        xr = P.dram("xr", [DM, S], F32)
        zs = P.dram("zs", [2048, S], BF16)
        Xtok = P.dram("Xtok", [S, 2048], BF16)
        Btok = P.dram("Btok", [S, 1024], BF16)
        BTd = P.dram("BTd", [1024, S], BF16)
        CTd = P.dram("CTd", [1024, S], BF16)
        yTd = P.dram("yTd", [2048, S], BF16)
        KTd = P.dram("KTd", [8, 2, 66, S], BF16)
        QTd = P.dram("QTd", [8, 2, 66, S], BF16)
        Vtok = P.dram("Vtok", [S, DM], BF16)
        GTd = P.dram("GTd", [DM, S], BF16)
        YTd = P.dram("YTd", [DM, S], BF16)

        ident_f = P.sb("ident_f", [128, 128], F32)
        ident_b = P.sb("ident_b", [128, 128], BF16)
        ones_b = P.sb("ones_b", [128, 128], BF16)
        ones_f = P.sb("ones_f", [128, 128], F32)
        ind_f = P.sb("ind_f", [128, 32], F32)
        indblk = P.sb("indblk", [128, 32, 64], BF16)
        negm_f = P.sb("negm_f", [128, 64], F32)
        onid_f = P.sb("onid_f", [128, 64], F32)
        onid_b = P.sb("onid_b", [128, 64], BF16)
        epsb = P.sb("epsb", [128, 1], F32)
        oneb = P.sb("oneb", [128, 1], F32)
        normg = P.sb("normg", [128, 2, 8], F32)
        convw = P.sb("convw", [128, 2, 32, 4], F32)
        convb = P.sb("convb", [128, 2, 32], F32)
        hp = P.sb("hp", [128, 2, 2], F32)
        dskip = P.sb("dskip", [128, 2, 32], F32)
        gng = P.sb("gng", [128, 2, 16], F32)
        kvg = P.sb("kvg", [128, 8], F32)
        bng = P.sb("bng", [128, 2, 8], F32)
        fing = P.sb("fing", [128, 8], F32)
        blam = P.sb("blam", [128, 2, 4, 64], F32)
        bsubg = P.sb("bsubg", [128, 2], F32)
        lamt = P.sb("lamt", [128, 8], F32)
        AL = P.sb("AL", [128, S], BF16)
        TMtok = P.sb("TMtok", [64, NCH, 32], BF16)
        Ebc = P.sb("Ebc", [128, NCH, 32], F32)
        ea = P.sb("ea", [128, 1], F32)
        ARENA_N = 84544
        arena_t = P.sb("arena", [128, ARENA_N], BF16)
        AR = Arena(arena_t, ARENA_N)

        def ld(dst, src, slot, writes):
            Sc.dma("sp", lambda e: e.dma_start(out=dst, in_=src), slot, writes=writes)

        ld(ident_f[:], c_ident[:, :], "c0", ["ident_f"])
        ld(ind_f[:], c_ind[:, :], "c1", ["ind_f"])
        ld(negm_f[:], c_negm[:, :], "c2", ["negm_f"])
        ld(normg[:], a_norm_g[:, :, :], "c3", ["normg"])
        ld(convw[:], a_conv_w[:, :, :, :], "c4", ["convw"])
        ld(convb[:], a_conv_b[:, :, :], "c5", ["convb"])
        ld(hp[:], a_hp[:, :, :], "c6", ["hp"])
        ld(dskip[:], a_dskip[:, :, :], "c7", ["dskip"])
        ld(gng[:], a_gng[:, :, :], "c8", ["gng"])
        ld(onid_f[:], c_onid[:, :], "c9", ["onid_f"])
        ld(kvg[:], kv_norm_g[:, :], "c10", ["normg"])
        ld(bng[:], b_norm_g[:, :, :], "c11", ["normg"])
        ld(fing[:], final_g[:, :], "c12", ["normg"])
        ld(blam[:], b_lam[:, :, :, :], "c13", ["blam"])
        ld(bsubg[:], b_subg[:, :], "c14", ["bsubg"])
        for h in range(8):
            for i in range(2):
                Sc.dma("sp", lambda e, h=h, i=i: e.dma_start(out=KTd[h, i, 64:66, :], in_=c_kaug[:, :]), "caug", writes=[("dram", "KTd")])
                Sc.dma("sp", lambda e, h=h, i=i: e.dma_start(out=QTd[h, i, 64:66, :], in_=c_qaug[h, :, :]), "caug", writes=[("dram", "QTd")])
        Sc.op("pool", lambda e: e.memset(ones_b[:], 1.0), writes=["ones_b"])
        Sc.op("pool", lambda e: e.memset(ones_f[:], 1.0), writes=["ones_f"])
        Sc.op("pool", lambda e: e.memset(epsb[:], EPS), writes=["epsb"])
        Sc.op("pool", lambda e: e.memset(oneb[:], 1.0), writes=["oneb"])
        Sc.op("dve", lambda e: e.tensor_copy(out=ident_b[:], in_=ident_f[:]), reads=["ident_f"], writes=["ident_b"])
        Sc.op("dve", lambda e: e.tensor_copy(out=onid_b[:], in_=onid_f[:]), reads=["onid_f"], writes=["onid_b"])
        Sc.op("dve", lambda e: e.tensor_copy(out=indblk[:], in_=bc_last(ind_f[:], 64)), reads=["ind_f"], writes=["indblk"])

        psA = P.ps("psA", [128, 2048], F32)
        psB = P.ps("psB", [128, 2048], F32)

        def rms_stage(src, g_ap_fn, hT, bufs):
            xt_buf, sq_buf, rs_buf = bufs
            srcv = src.rearrange("(c p) t -> p c t", p=128)
            for tt in range(8):
                xb = xt_buf[tt % 2]
                xname = "xt%d" % (tt % 2)
                tsl = slice(tt * 512, (tt + 1) * 512)
                Sc.dma("sp", lambda e, xb=xb, tsl=tsl: e.dma_start(out=xb, in_=srcv[:, :, tsl]),
                       "ld_" + xname, reads=[("dram", src.name, 2 * tt), ("dram", src.name, 2 * tt + 1)], writes=[xname])
                Sc.op("act", lambda e, xb=xb: e.activation(out=sq_buf, in_=xb, func=AF.Square),
                      reads=[xname], writes=["sq"])
                for c in range(8):
                    Sc.op("pe", lambda e, c=c: e.matmul(psA[:, 0:512], lhsT=ones_b[:], rhs=sq_buf[:, c, :],
                                                       start=(c == 0), stop=(c == 7)),
                          reads=["sq", "ones_b"], writes=["psA0"])
                rs = rs_buf[tt % 2]
                rname = "rs%d" % (tt % 2)
                Sc.op("act", lambda e, rs=rs: e.activation(out=rs, in_=psA[:, 0:512], func=AF.Ln,
                                                            scale=1.0 / DM, bias=epsb[:]),
                      reads=["psA0", "epsb"], writes=[rname])
                Sc.op("act", lambda e, rs=rs: e.activation(out=rs, in_=rs, func=AF.Exp, scale=-0.5),
                      reads=[rname], writes=[rname])
                for c in range(8):
                    Sc.op("dve", lambda e, c=c, xb=xb, rs=rs, tsl=tsl: e.scalar_tensor_tensor(
                        out=hT[:, c, tsl], in0=xb[:, c, :], scalar=g_ap_fn(c), in1=rs,
                        op0=ALU.mult, op1=ALU.mult),
                        reads=[xname, rname, "normg"], writes=[("hT", tt)])

        src_x = xT_in
        for L in range(n_mamba):
            Sc.barrier()
            AR.reset()
            hT = AR.alloc([128, 8, S], BF16)
            offB = AR.off
            xt_buf = [AR.alloc([128, 8, 512], F32) for i in range(2)]
            sq_buf = AR.alloc([128, 8, 512], BF16)
            rs_buf = [AR.alloc([128, 512], F32) for i in range(2)]
            rms_stage(src_x, lambda c, L=L: normg[:, L, c:c + 1], hT, (xt_buf, sq_buf, rs_buf))

            Sc.barrier()
            AR.reset(offB)
            wst = [AR.alloc([128, 8, 128], F32) for i in range(2)]
            wbf = [AR.alloc([128, 8, 128], BF16) for i in range(2)]
            U = [AR.alloc([128, 3 + S + 13], BF16)]
            XO = [AR.alloc([128, S], BF16) for i in range(2)]
            XTk = [AR.alloc([128, 32, 128], BF16)]
            dg = [AR.alloc([128, 4, 128], BF16)]
            bufA = AR.alloc([128, S], F32)
            bufB = AR.alloc([128, S], F32)
            bufC = AR.alloc([128, S], F32)
            hiB = AR.alloc([128, S], BF16)
            segm = hiB
            Eblk = XO[1][0:32, :].bitcast(F32).rearrange("p (a b) -> p a b", b=32)
            Sc.op("pool", lambda e: e.memset(U[0][:, 0:3], 0.0), writes=["U0"])
            Sc.dma("sp", lambda e: e.dma_start(out=segm, in_=c_segm[:, :]), "ld_segm", writes=["hiB"])

            win = a_w_in[L]
            winv = win.rearrange("(c p) n -> p c n", p=128)
            order = [("dt", 0)] + [("B", g) for g in range(8)] + [("C", g) for g in range(8)] + \
                    [("x", j) for j in range(16)] + [("z", j) for j in range(16)]
            def prepB(ci):
                kind, j = order[ci]
                wb = ci % 2
                ws, wbt = wst[wb], wbf[wb]
                wsn, wbn = "wst%d" % wb, "wbf%d" % wb
                if kind == "dt":
                    for rep in range(4):
                        Sc.dma("sp", lambda e, ws=ws, rep=rep: e.dma_start(
                            out=ws[:, :, rep * 32:(rep + 1) * 32], in_=winv[:, :, 6144:6176]),
                            "ld_" + wsn, writes=[wsn])
                else:
                    col0 = {"z": 0, "x": 2048, "B": 4096, "C": 5120}[kind] + j * 128
                    Sc.dma("sp", lambda e, ws=ws, col0=col0: e.dma_start(out=ws, in_=winv[:, :, col0:col0 + 128]),
                           "ld_" + wsn, writes=[wsn])
                Sc.op("dve", lambda e, ws=ws, wbt=wbt: e.tensor_copy(out=wbt, in_=ws), reads=[wsn], writes=[wbn])
                if kind in ("x", "B", "C"):
                    cch = {"x": 0, "B": 16, "C": 24}[kind] + j
                    for tap in range(4):
                        Sc.op("dve", lambda e, tap=tap, cch=cch: e.tensor_scalar(
                            out=dg[0][:, tap, :], in0=ident_f[:], scalar1=convw[:, L, cch, tap:tap + 1], scalar2=None,
                            op0=ALU.mult), reads=["ident_f", "convw"], writes=["dg0"])

            prepB(0)
            for ci, (kind, j) in enumerate(order):
                wb = ci % 2
                ws, wbt = wst[wb], wbf[wb]
                wsn, wbn = "wst%d" % wb, "wbf%d" % wb
                ub = ci % 2
                Ub, XOb = U[0], XO[ub]
                Un, XOn = "U0", "XO%d" % ub
                conv = kind in ("x", "B", "C")
                if conv:
                    cch = {"x": 0, "B": 16, "C": 24}[kind] + j
                    dgb = dg[0]
                    dgn = "dg0"
                for tt in range(8):
                    bank = tt % 4
                    pst = psA[:, bank * 512:(bank + 1) * 512]
                    pn = "psA%d" % bank
                    for c in range(8):
                        Sc.op("pe", lambda e, pst=pst, c=c, tt=tt, wbt=wbt: e.matmul(
                            pst, lhsT=wbt[:, c, :], rhs=hT[:, c, tt * 512:(tt + 1) * 512], start=(c == 0), stop=(c == 7)),
                            reads=[wbn, ("hT", tt)], writes=[pn])
                    tsl = slice(tt * 512, (tt + 1) * 512)
                    if kind == "z":
                        Sc.op("act", lambda e, pst=pst, tsl=tsl, XOb=XOb: e.activation(out=XOb[:, tsl], in_=pst, func=AF.Silu),
                              reads=[pn], writes=[XOn])
                    elif kind == "dt":
                        Sc.op("act", lambda e, pst=pst, tsl=tsl: e.activation(out=bufA[:, tsl], in_=pst, func=AF.Exp,
                                                                                 bias=hp[:, L, 0:1], scale=1.0),
                              reads=[pn, "hp"], writes=["bufA"])
                    else:
                        Sc.op("dve", lambda e, pst=pst, tt=tt, Ub=Ub: e.tensor_copy(out=Ub[:, 3 + tt * 512:3 + (tt + 1) * 512], in_=pst),
                              reads=[pn], writes=[Un])
                if conv:
                    for tt in range(8):
                        bank = tt % 4
                        pst = psB[:, bank * 512:(bank + 1) * 512]
                        pn = "psB%d" % bank
                        for tap in range(4):
                            Sc.op("pe", lambda e, pst=pst, tap=tap, tt=tt, Ub=Ub, dgb=dgb: e.matmul(
                                pst, lhsT=dgb[:, tap, :], rhs=Ub[:, tt * 512 + tap:tt * 512 + tap + 512],
                                start=(tap == 0), stop=(tap == 3)), reads=[dgn, Un], writes=[pn])
                        tsl = slice(tt * 512, (tt + 1) * 512)
                        Sc.op("act", lambda e, pst=pst, tsl=tsl, XOb=XOb, cch=cch: e.activation(
                            out=XOb[:, tsl], in_=pst, func=AF.Silu, bias=convb[:, L, cch:cch + 1], scale=1.0),
                            reads=[pn, "convb"], writes=[XOn])
                if ci + 1 < len(order):
                    prepB(ci + 1)
                if kind == "z":
                    Sc.dma("pool", lambda e, XOb=XOb, j=j: e.dma_start(out=zs[j * 128:(j + 1) * 128, :], in_=XOb),
                           "st_" + XOn, reads=[XOn], writes=[("dram", "zs")])
                elif kind in ("B", "C"):
                    dst = BTd if kind == "B" else CTd
                    Sc.dma("pool", lambda e, XOb=XOb, j=j, dst=dst: e.dma_start(out=dst[j * 128:(j + 1) * 128, :], in_=XOb),
                           "st_" + XOn, reads=[XOn], writes=[("dram", dst.name)])
                if kind in ("x", "B"):
                    XTb = XTk[0]
                    XTn = "XTk0"
                    for q in range(4):
                        bank = q % 2
                        pbt = psB[:, bank * 512:bank * 512 + 512].bitcast(BF16)
                        pn = "psB%d" % bank
                        for i8 in range(8):
                            tb = q * 8 + i8
                            Sc.op("pe", lambda e, pbt=pbt, i8=i8, tb=tb, XOb=XOb: e.transpose(
                                pbt[:, i8 * 128:(i8 + 1) * 128], XOb[:, tb * 128:(tb + 1) * 128], ident_b[:]),
                                reads=[XOn, "ident_b"], writes=[pn])
                        Sc.op("dve", lambda e, pbt=pbt, q=q, XTb=XTb: e.tensor_copy(
                            out=XTb[:, q * 8:(q + 1) * 8, :], in_=pbt.rearrange("p (a b) -> p a b", b=128)),
                            reads=[pn], writes=[XTn])
                    dst = Xtok if kind == "x" else Btok
                    dstv = dst.rearrange("(tb p) c -> p tb c", p=128)
                    Sc.dma("pool", lambda e, XTb=XTb, dstv=dstv, j=j: e.dma_start(
                        out=dstv[:, :, j * 128:(j + 1) * 128], in_=XTb),
                        "st_" + XTn, reads=[XTn], writes=[("dram", dst.name)])
                if kind == "dt":
                    Sc.op("act", lambda e: e.activation(out=bufA, in_=bufA, func=AF.Ln, bias=oneb[:], scale=1.0),
                          reads=["bufA", "oneb"], writes=["bufA"])
                    Sc.op("act", lambda e: e.activation(out=ea[:], in_=hp[:, L, 1:2], func=AF.Exp),
                          reads=["hp"], writes=["ea"])
                    Sc.op("dve", lambda e: e.tensor_scalar(out=bufB, in0=bufA, scalar1=ea[:, 0:1], scalar2=-1.0,
                                                           op0=ALU.mult, op1=ALU.mult),
                          reads=["bufA", "ea"], writes=["bufB"])
                    Sc.op("dve", lambda e: e.tensor_tensor_scan(out=bufC, data0=segm, data1=bufB, initial=0.0,
                                                                op0=ALU.mult, op1=ALU.add),
                          reads=["hiB", "bufB"], writes=["bufC"])
                    Sc.op("act", lambda e: e.activation(out=bufB[64:128, :], in_=bufA[64:128, :], func=AF.Ln),
                          reads=["bufA", "bufB"], writes=["bufB"])
                    Sc.op("dve", lambda e: e.tensor_tensor(out=bufC[64:128, :], in0=bufC[64:128, :], in1=bufB[64:128, :], op=ALU.subtract),
                          reads=["bufC", "bufB"], writes=["bufC"])
                    Sc.op("dve", lambda e: e.tensor_copy(out=hiB, in_=bufC), reads=["bufC"], writes=["hiB"])
                    Sc.op("dve", lambda e: e.tensor_copy(out=AL[0:32, :], in_=hiB[0:32, :]), reads=["hiB"], writes=["AL"])
                    Sc.op("dve", lambda e: e.tensor_tensor(out=AL[32:64, :], in0=bufC[32:64, :], in1=hiB[32:64, :], op=ALU.subtract),
                          reads=["bufC", "hiB"], writes=["AL"])
                    Sc.op("dve", lambda e: e.tensor_scalar(out=AL[64:96, :], in0=hiB[64:96, :], scalar1=-1.0, scalar2=None, op0=ALU.mult),
                          reads=["hiB"], writes=["AL"])
                    Sc.op("dve", lambda e: e.tensor_tensor(out=AL[96:128, :], in0=hiB[96:128, :], in1=bufC[96:128, :], op=ALU.subtract),
                          reads=["bufC", "hiB"], writes=["AL"])
                    Sc.op("act", lambda e: e.activation(out=bufA[0:32, :], in_=bufC[0:32, :], func=AF.Exp),
                          reads=["bufC", "bufB"], writes=["bufA"])
                    for c in range(NCH):
                        pst = psB[0:64, (c % 4) * 512:(c % 4) * 512 + 32]
                        pn = "psB%d" % (c % 4)
                        Sc.op("pe", lambda e, pst=pst, c=c: e.transpose(pst, bufA[0:32, c * 64:(c + 1) * 64], ident_f[0:32, 0:32]),
                              reads=["bufA", "ident_f"], writes=[pn])
                        Sc.op("dve", lambda e, pst=pst, c=c: e.tensor_copy(out=TMtok[:, c, :], in_=pst), reads=[pn], writes=["TMtok"])
                    a0 = bufC[0:32, :].rearrange("p (c l) -> p c l", l=64)
                    Sc.op("act", lambda e: e.activation(out=bufB[0:32, 0:NCH], in_=a0[:, :, 63], func=AF.Exp),
                          reads=["bufC", "bufB"], writes=["bufB"])
                    Sc.op("dve", lambda e: e.tensor_tensor(out=Eblk, in0=bc_last(bufB[0:32, 0:NCH], 32),
                                                           in1=bc_mid(ind_f[0:32, :], NCH), op=ALU.mult),
                          reads=["bufB", "ind_f"], writes=["Eblk"])
                    for q in range(4):
                        pst = psB[:, q * 512:(q + 1) * 512]
                        pn = "psB%d" % q
                        Sc.op("pe", lambda e, pst=pst, q=q: e.matmul(
                            pst, lhsT=ones_f[0:32, :], rhs=Eblk[:, q * 16:(q + 1) * 16, :].rearrange("p a b -> p (a b)"),
                            start=True, stop=True), reads=["Eblk", "ones_f"], writes=[pn])
                        Sc.op("act", lambda e, pst=pst, q=q: e.activation(
                            out=Ebc[:, q * 16:(q + 1) * 16, :].rearrange("p a b -> p (a b)"), in_=pst, func=AF.Copy),
                            reads=[pn], writes=["Ebc"])
            if stop_after == ("B", L):
                break

            Sc.barrier()
            AR.reset()
            St = AR.alloc([128, 2048], F32)
            Sbf = AR.alloc([128, 2048], BF16)
            Xq = [AR.alloc([64, 4, 2048], BF16) for i in range(2)]
            Bq = [AR.alloc([64, 4, 1024], BF16) for i in range(2)]
            BTq = [AR.alloc([128, 8, 256], BF16) for i in range(2)]
            CTq = [AR.alloc([128, 8, 256], BF16) for i in range(2)]
            RH = [AR.alloc([128, 32, 64], BF16) for i in range(2)]
            Dc = AR.alloc([64, 32, 64], BF16)
            MT = AR.alloc([64, 32, 64], BF16)
            Xw = AR.alloc([64, 32, 64], BF16)
            XD = [AR.alloc([64, 32, 64], BF16) for i in range(2)]
            t1 = AR.alloc([64, 32, 64], BF16)
            te32 = AR.alloc([64, 32], BF16)
            y3a = AR.alloc([64, 32, 64], BF16)
            y3 = AR.alloc([64, 2048], BF16)
            yTq = [AR.alloc([128, 16, 256], BF16) for i in range(2)]
            Sc.op("pool", lambda e: e.memset(St, 0.0), writes=["St0", "St1", "St2", "St3"])
            Sc.op("pool", lambda e: e.memset(Sbf, 0.0), writes=["Sbf0", "Sbf1", "Sbf2", "Sbf3"])
            for k in range(2):
                Sc.op("dve", lambda e, k=k: e.tensor_copy(out=RH[k][64:128, :, :], in_=bc_mid(negm_f[64:128, :], 32)),
                      reads=["negm_f"], writes=["RH%d" % k])
            Xtv = Xtok.rearrange("(q j p) c -> q p j c", j=4, p=64)
            Btv = Btok.rearrange("(q j p) c -> q p j c", j=4, p=64)
            BTv = BTd.rearrange("(g n) t -> n g t", n=128)
            CTv = CTd.rearrange("(g n) t -> n g t", n=128)
            yTv = yTd.rearrange("(c p) t -> p c t", p=128)

            def loadC(q):
                qb = q % 2
                Xn, Bn, BTn, CTn = "Xq%d" % qb, "Bq%d" % qb, "BTq%d" % qb, "CTq%d" % qb
                Sc.dma("sp", lambda e, q=q, qb=qb: e.dma_start(out=Xq[qb], in_=Xtv[q]), "ld_" + Xn,
                       reads=[("dram", "Xtok")], writes=[Xn])
                Sc.dma("sp", lambda e, q=q, qb=qb: e.dma_start(out=Bq[qb], in_=Btv[q]), "ld_" + Bn,
                       reads=[("dram", "Btok")], writes=[Bn])
                Sc.dma("sp", lambda e, q=q, qb=qb: e.dma_start(out=BTq[qb], in_=BTv[:, :, q * 256:(q + 1) * 256]), "ld_" + BTn,
                       reads=[("dram", "BTd")], writes=[BTn])
                Sc.dma("sp", lambda e, q=q, qb=qb: e.dma_start(out=CTq[qb], in_=CTv[:, :, q * 256:(q + 1) * 256]), "ld_" + CTn,
                       reads=[("dram", "CTd")], writes=[CTn])

            def prepC(c):
                q, jq = c // 4, c % 4
                qb = q % 2
                if jq == 0:
                    loadC(q)
                Xc = Xq[qb][:, jq, :].rearrange("p (h d) -> p h d", d=64)
                tsl = slice(c * 64, (c + 1) * 64)
                rb = c % 2
                Sc.op("pool", lambda e, rb=rb, tsl=tsl: e.tensor_tensor(
                    out=RH[rb][0:64, :, :], in0=bc_mid(AL[0:64, tsl], 32), in1=indblk[0:64, :, :], op=ALU.mult),
                    reads=["AL", "indblk"], writes=["RH%d" % rb])
                Sc.op("pool", lambda e, Xc=Xc, rb=rb: e.tensor_tensor(out=XD[rb], in0=Xc, in1=bc_last(dskip[0:64, L, :], 64), op=ALU.mult),
                      reads=["Xq%d" % qb, "dskip"], writes=["XD%d" % rb])

            def emitD(c):
                tsl = slice(c * 64, (c + 1) * 64)
                RHb, RHn = RH[c % 2], "RH%d" % (c % 2)
                for k4 in range(4):
                    pst = psA[0:64, k4 * 512:(k4 + 1) * 512]
                    pn = "psA%d" % k4
                    rsl = slice(k4 * 8, (k4 + 1) * 8)
                    Sc.op("pe", lambda e, pst=pst, RHb=RHb, rsl=rsl: e.matmul(
                        pst, lhsT=onid_b[:, :], rhs=RHb[:, rsl, :].rearrange("p a b -> p (a b)"), start=True, stop=False),
                        reads=[RHn, "onid_b"], writes=[pn])
                    Sc.op("pe", lambda e, pst=pst, rsl=rsl, tsl=tsl: e.matmul(
                        pst, lhsT=AL[64:128, tsl], rhs=indblk[64:128, rsl, :].rearrange("p a b -> p (a b)"), start=False, stop=True),
                        reads=["AL", "indblk"], writes=[pn])
                Sc.op("act", lambda e: e.activation(out=Dc.rearrange("p a b -> p (a b)"), in_=psA[0:64, :], func=AF.Exp),
                      reads=["psA0", "psA1", "psA2", "psA3"], writes=["Dc"])

            prepC(0)
            emitD(0)
            for c in range(NCH):
                q, jq = c // 4, c % 4
                qb = q % 2
                Xn, Bn, BTn, CTn = "Xq%d" % qb, "Bq%d" % qb, "BTq%d" % qb, "CTq%d" % qb
                Xc = Xq[qb][:, jq, :].rearrange("p (h d) -> p h d", d=64)
                Bc = Bq[qb][:, jq, :]
                csl = slice(jq * 64, (jq + 1) * 64)
                tsl = slice(c * 64, (c + 1) * 64)
                rb = c % 2
                RHb, RHn = RH[rb], "RH%d" % rb
                XDb, XDn = XD[rb], "XD%d" % rb
                for g in range(8):
                    Sc.op("pe", lambda e, g=g, qb=qb, csl=csl: e.matmul(
                        psB[0:64, g * 64:(g + 1) * 64], lhsT=BTq[qb][:, g, csl], rhs=CTq[qb][:, g, csl], start=True, stop=True),
                        reads=[BTn, CTn], writes=["psB0"])
                if c + 1 < NCH:
                    prepC(c + 1)
                for k4 in range(4):
                    gsl = slice(2 * k4, 2 * k4 + 2)
                    cbv = psB[0:64, 0:512].rearrange("p (g l) -> p g l", l=64)[:, gsl, :].unsqueeze(2).to_broadcast([64, 2, 4, 64])
                    Sc.op("dve", lambda e, cbv=cbv, gsl=gsl: e.tensor_tensor(
                        out=MT.rearrange("p (g r) l -> p g r l", r=4)[:, gsl], in0=Dc.rearrange("p (g r) l -> p g r l", r=4)[:, gsl],
                        in1=cbv, op=ALU.mult), reads=["Dc", "psB0"], writes=["MT%d" % k4])
                Sc.op("dve", lambda e: e.tensor_copy(out=te32, in_=Dc[:, :, 63]), reads=["Dc"], writes=["te32"])
                for h in range(32):
                    pst = psA[0:64, h * 64:(h + 1) * 64]
                    pn = "psA%d" % (h // 8)
                    Sc.op("pe", lambda e, pst=pst, h=h, Xc=Xc: e.matmul(pst, lhsT=MT[:, h, :], rhs=Xc[:, h, :], start=True, stop=True),
                          reads=["MT%d" % (h // 8), Xn], writes=[pn])
                for g in range(8):
                    pst = psB[0:64, g * 256:(g + 1) * 256]
                    pn = "psB%d" % (g // 2)
                    Sc.op("pe", lambda e, pst=pst, g=g, qb=qb, csl=csl: e.matmul(
                        pst, lhsT=CTq[qb][:, g, csl], rhs=Sbf[:, g * 256:(g + 1) * 256], start=True, stop=True),
                        reads=[CTn, "Sbf%d" % (g // 2)], writes=[pn])
                for k4 in range(4):
                    Sc.op("dve", lambda e, XDb=XDb, k4=k4: e.tensor_tensor(
                        out=y3a[:, k4 * 8:(k4 + 1) * 8, :], in0=psA[0:64, k4 * 512:(k4 + 1) * 512].rearrange("p (h d) -> p h d", d=64),
                        in1=XDb[:, k4 * 8:(k4 + 1) * 8, :], op=ALU.add),
                        reads=["psA%d" % k4, XDn], writes=["y3a%d" % k4])
                if c + 1 < NCH:
                    emitD(c + 1)
                Sc.op("dve", lambda e, c=c: e.tensor_tensor(out=t1, in0=psB[0:64, :].rearrange("p (h d) -> p h d", d=64),
                                                           in1=bc_last(TMtok[:, c, :], 64), op=ALU.mult),
                      reads=["psB0", "psB1", "psB2", "psB3", "TMtok"], writes=["t1"])
                Sc.op("dve", lambda e: e.tensor_tensor(out=y3, in0=y3a.rearrange("p a b -> p (a b)"), in1=t1.rearrange("p a b -> p (a b)"), op=ALU.add),
                      reads=["y3a0", "y3a1", "y3a2", "y3a3", "t1"], writes=["y3"])
                Sc.op("dve", lambda e, Xc=Xc: e.tensor_tensor(out=Xw, in0=Xc, in1=bc_last(te32, 64), op=ALU.mult),
                      reads=[Xn, "te32"], writes=["Xw"])
                pbt = psB[:, 0:512].bitcast(BF16)
                for cc in range(16):
                    Sc.op("pe", lambda e, cc=cc: e.transpose(pbt[:, cc * 64:(cc + 1) * 64], y3[:, cc * 128:(cc + 1) * 128], ident_b[0:64, 0:64]),
                          reads=["y3", "ident_b"], writes=["psB0"])
                yb = yTq[qb]
                yn = "yTq%d" % qb
                Sc.op("act", lambda e, yb=yb, csl=csl: e.activation(out=yb[:, :, csl], in_=pbt.rearrange("p (a b) -> p a b", b=64), func=AF.Copy),
                      reads=["psB0"], writes=[yn])
                if jq == 3:
                    Sc.dma("pool", lambda e, yb=yb, q=q: e.dma_start(out=yTv[:, :, q * 256:(q + 1) * 256], in_=yb), "st_" + yn,
                           reads=[yn], writes=[("dram", "yTd")])
                for g in range(8):
                    pst = psB[:, g * 256:(g + 1) * 256]
                    pn = "psB%d" % (g // 2)
                    Sc.op("pe", lambda e, pst=pst, g=g, Bc=Bc: e.matmul(
                        pst, lhsT=Bc[:, g * 128:(g + 1) * 128], rhs=Xw.rearrange("p a b -> p (a b)")[:, g * 256:(g + 1) * 256],
                        start=True, stop=True), reads=[Bn, "Xw"], writes=[pn])
                Sc.op("pool", lambda e, c=c: e.tensor_tensor(
                    out=St.rearrange("p (h d) -> p h d", d=64), in0=St.rearrange("p (h d) -> p h d", d=64),
                    in1=bc_last(Ebc[:, c, :], 64), op=ALU.mult), reads=["St0", "St1", "St2", "St3"] + ["Ebc"], writes=["St0", "St1", "St2", "St3"])
                for k4 in range(4):
                    Sc.op("dve", lambda e, k4=k4: e.tensor_tensor(out=St[:, k4 * 512:(k4 + 1) * 512], in0=St[:, k4 * 512:(k4 + 1) * 512],
                                                               in1=psB[:, k4 * 512:(k4 + 1) * 512], op=ALU.add),
                          reads=["St%d" % k4, "psB%d" % k4], writes=["St%d" % k4])
                for k4 in range(4):
                    Sc.op("act", lambda e, k4=k4: e.activation(out=Sbf[:, k4 * 512:(k4 + 1) * 512], in_=St[:, k4 * 512:(k4 + 1) * 512], func=AF.Copy),
                          reads=["St%d" % k4], writes=["Sbf%d" % k4])
            if stop_after == ("C", L):
                break

            Sc.barrier()
            AR.reset()
            wo = AR.alloc([128, 16, DM], BF16)
            wos = [AR.alloc([128, 2, DM], F32)]
            yt = [AR.alloc([128, 16, 256], BF16) for i in range(2)]
            zt = [AR.alloc([128, 16, 256], BF16) for i in range(2)]
            ut = [AR.alloc([128, 16, 256], BF16) for i in range(2)]
            usq = [AR.alloc([128, 16, 256], BF16) for i in range(2)]
            rsd = AR.alloc([128, 8, 256], F32)
            un = [AR.alloc([128, 16, 256], BF16) for i in range(2)]
            xres = [AR.alloc([128, 8, 256], F32) for i in range(2)]
            wov = a_w_out[L].rearrange("(c p) d -> p c d", p=128)
            for k8 in range(8):
                wsb = wos[0]
                wsn = "wos0"
                Sc.dma("sp", lambda e, wsb=wsb, k8=k8: e.dma_start(out=wsb, in_=wov[:, k8 * 2:(k8 + 1) * 2, :]), "ld_" + wsn, writes=[wsn])
                Sc.op("dve" if k8 % 2 == 0 else "act", lambda e, wsb=wsb, k8=k8: (
                    e.tensor_copy(out=wo[:, k8 * 2:(k8 + 1) * 2, :], in_=wsb) if k8 % 2 == 0 else
                    e.activation(out=wo[:, k8 * 2:(k8 + 1) * 2, :], in_=wsb, func=AF.Copy)), reads=[wsn], writes=["wo"])
            zsv = zs.rearrange("(c p) t -> p c t", p=128)
            srcv = src_x.rearrange("(c p) t -> p c t", p=128)
            xrv = xr.rearrange("(c p) t -> p c t", p=128)

            def frontD(tt):
                b2 = tt % 2
                tsl = slice(tt * 256, (tt + 1) * 256)
                Sc.dma("sp", lambda e, b2=b2, tsl=tsl: e.dma_start(out=yt[b2], in_=yTv[:, :, tsl]), "ld_yt%d" % b2,
                       reads=[("dram", "yTd")], writes=["yt%d" % b2])
                Sc.dma("sp", lambda e, b2=b2, tsl=tsl: e.dma_start(out=zt[b2], in_=zsv[:, :, tsl]), "ld_zt%d" % b2,
                       reads=[("dram", "zs")], writes=["zt%d" % b2])
                Sc.dma("sp", lambda e, b2=b2, tsl=tsl: e.dma_start(out=xres[b2], in_=srcv[:, :, tsl]), "ld_xres%d" % b2,
                       reads=[("dram", src_x.name, tt)], writes=["xres%d" % b2])
                Sc.op("dve", lambda e, b2=b2: e.tensor_tensor(out=ut[b2], in0=yt[b2], in1=zt[b2], op=ALU.mult),
                      reads=["yt%d" % b2, "zt%d" % b2], writes=["ut%d" % b2])
                Sc.op("act", lambda e, b2=b2: e.activation(out=usq[b2], in_=ut[b2], func=AF.Square), reads=["ut%d" % b2], writes=["usq%d" % b2])

            def midD(tt):
                b2 = tt % 2
                for g in range(8):
                    pst = psA[:, g * 256:(g + 1) * 256]
                    pn = "psA%d" % (g // 2)
                    for k2 in range(2):
                        Sc.op("pe", lambda e, pst=pst, g=g, k2=k2, b2=b2: e.matmul(pst, lhsT=ones_b[:], rhs=usq[b2][:, 2 * g + k2, :],
                                                                                    start=(k2 == 0), stop=(k2 == 1)),
                              reads=["usq%d" % b2, "ones_b"], writes=[pn])
                Sc.op("act", lambda e: e.activation(out=rsd.rearrange("p a b -> p (a b)"), in_=psA[:, :], func=AF.Ln,
                                                    scale=1.0 / 256.0, bias=epsb[:]),
                      reads=["psA0", "psA1", "psA2", "psA3", "epsb"], writes=["rsd"])
                Sc.op("act", lambda e: e.activation(out=rsd, in_=rsd, func=AF.Exp, scale=-0.5), reads=["rsd"], writes=["rsd"])
                for cc in range(16):
                    Sc.op("dve", lambda e, cc=cc, b2=b2: e.scalar_tensor_tensor(
                        out=un[b2][:, cc, :], in0=ut[b2][:, cc, :], scalar=gng[:, L, cc:cc + 1], in1=rsd[:, cc // 2, :],
                        op0=ALU.mult, op1=ALU.mult), reads=["ut%d" % b2, "rsd", "gng"], writes=["un%d" % b2])

            def backD(tt):
                b2 = tt % 2
                tsl = slice(tt * 256, (tt + 1) * 256)
                xrb, xrn = xres[b2], "xres%d" % b2
                for dc in range(8):
                    pst = psB[:, (dc % 4) * 512:(dc % 4) * 512 + 256]
                    pn = "psB%d" % (dc % 4)
                    for cc in range(16):
                        Sc.op("pe", lambda e, pst=pst, dc=dc, cc=cc, b2=b2: e.matmul(
                            pst, lhsT=wo[:, cc, dc * 128:(dc + 1) * 128], rhs=un[b2][:, cc, :], start=(cc == 0), stop=(cc == 15)),
                            reads=["wo", "un%d" % b2], writes=[pn])
                    Sc.op("dve", lambda e, pst=pst, dc=dc, xrb=xrb: e.tensor_tensor(out=xrb[:, dc, :], in0=xrb[:, dc, :], in1=pst, op=ALU.add),
                          reads=[pn, xrn], writes=[xrn])
                Sc.dma("pool", lambda e, xrb=xrb, tsl=tsl: e.dma_start(out=xrv[:, :, tsl], in_=xrb), "st_" + xrn,
                       reads=[xrn], writes=[("dram", "xr", tt)])

            frontD(0)
            midD(0)
            frontD(1)
            for tt in range(16):
                if tt + 1 < 16:
                    midD(tt + 1)
                backD(tt)
                if tt + 2 < 16:
                    frontD(tt + 2)
            src_x = xr


        def proj_stage(hT, wv, specs, offB):
            Sc.barrier()
            AR.reset(offB)
            wst = [AR.alloc([128, 8, 128], F32) for i in range(2)]
            wbf = [AR.alloc([128, 8, 128], BF16) for i in range(2)]
            XO = [AR.alloc([128, S], BF16) for i in range(2)]
            XTk = [AR.alloc([128, 32, 128], BF16)]
            def prepP(ci):
                col0 = specs[ci][0]
                wb = ci % 2
                ws, wbt = wst[wb], wbf[wb]
                wsn, wbn = "wst%d" % wb, "wbf%d" % wb
                Sc.dma("sp", lambda e, ws=ws, col0=col0: e.dma_start(out=ws, in_=wv[:, :, col0:col0 + 128]),
                       "ld_" + wsn, writes=[wsn])
                Sc.op("dve", lambda e, ws=ws, wbt=wbt: e.tensor_copy(out=wbt, in_=ws), reads=[wsn], writes=[wbn])

            prepP(0)
            for ci, (col0, evac, dst) in enumerate(specs):
                wb = ci % 2
                ws, wbt = wst[wb], wbf[wb]
                wsn, wbn = "wst%d" % wb, "wbf%d" % wb
                XOb, XOn = XO[ci % 2], "XO%d" % (ci % 2)
                for tt in range(8):
                    bank = tt % 4
                    pst = psA[:, bank * 512:(bank + 1) * 512]
                    pn = "psA%d" % bank
                    for c in range(8):
                        Sc.op("pe", lambda e, pst=pst, c=c, tt=tt, wbt=wbt: e.matmul(
                            pst, lhsT=wbt[:, c, :], rhs=hT[:, c, tt * 512:(tt + 1) * 512], start=(c == 0), stop=(c == 7)),
                            reads=[wbn, ("hT", tt)], writes=[pn])
                    tsl = slice(tt * 512, (tt + 1) * 512)
                    if evac == "silu":
                        Sc.op("act", lambda e, pst=pst, tsl=tsl, XOb=XOb: e.activation(out=XOb[:, tsl], in_=pst, func=AF.Silu),
                              reads=[pn], writes=[XOn])
                    elif evac == "q":
                        Sc.op("act", lambda e, pst=pst, tsl=tsl, XOb=XOb: e.activation(out=XOb[:, tsl], in_=pst, func=AF.Copy, scale=0.125),
                              reads=[pn], writes=[XOn])
                    else:
                        Sc.op("dve", lambda e, pst=pst, tsl=tsl, XOb=XOb: e.tensor_copy(out=XOb[:, tsl], in_=pst),
                              reads=[pn], writes=[XOn])
                if ci + 1 < len(specs):
                    prepP(ci + 1)
                kind, dt_, j = dst
                if kind == "feat":
                    Sc.dma("pool", lambda e, XOb=XOb, j=j, dt_=dt_: e.dma_start(out=dt_[j * 128:(j + 1) * 128, :], in_=XOb),
                           "st_" + XOn, reads=[XOn], writes=[("dram", dt_.name)])
                elif kind == "head2":
                    for i in range(2):
                        Sc.dma("pool", lambda e, XOb=XOb, j=j, dt_=dt_, i=i: e.dma_start(
                            out=dt_[j, i, 0:64, :], in_=XOb[i * 64:(i + 1) * 64, :]),
                            "st_" + XOn, reads=[XOn], writes=[("dram", dt_.name)])
                else:
                    XTb, XTn = XTk[0], "XTk0"
                    for q in range(4):
                        bank = q % 2
                        pbt = psB[:, bank * 512:bank * 512 + 512].bitcast(BF16)
                        pn = "psB%d" % bank
                        for i8 in range(8):
                            tb = q * 8 + i8
                            Sc.op("pe", lambda e, pbt=pbt, i8=i8, tb=tb, XOb=XOb: e.transpose(
                                pbt[:, i8 * 128:(i8 + 1) * 128], XOb[:, tb * 128:(tb + 1) * 128], ident_b[:]),
                                reads=[XOn, "ident_b"], writes=[pn])
                        Sc.op("dve", lambda e, pbt=pbt, q=q, XTb=XTb: e.tensor_copy(
                            out=XTb[:, q * 8:(q + 1) * 8, :], in_=pbt.rearrange("p (a b) -> p a b", b=128)),
                            reads=[pn], writes=[XTn])
                    dstv = dt_.rearrange("(tb p) c -> p tb c", p=128)
                    Sc.dma("pool", lambda e, XTb=XTb, dstv=dstv, j=j: e.dma_start(
                        out=dstv[:, :, j * 128:(j + 1) * 128], in_=XTb),
                        "st_" + XTn, reads=[XTn], writes=[("dram", dt_.name)])

        def norm_alloc():
            Sc.barrier()
            AR.reset()
            hT = AR.alloc([128, 8, S], BF16)
            offB = AR.off
            xt_buf = [AR.alloc([128, 8, 512], F32) for i in range(2)]
            sq_buf = AR.alloc([128, 8, 512], BF16)
            rs_buf = [AR.alloc([128, 512], F32) for i in range(2)]
            return hT, offB, (xt_buf, sq_buf, rs_buf)

        if n_attn > 0:
            hT, offB, bufs = norm_alloc()
            rms_stage(src_x, lambda c: kvg[:, c:c + 1], hT, bufs)
            wkvv = w_kv.rearrange("(c p) n -> p c n", p=128)
            specs = [(h * 128, "copy", ("head2", KTd, h)) for h in range(8)] + \
                    [(1024 + h * 128, "copy", ("tok", Vtok, h)) for h in range(8)]
            proj_stage(hT, wkvv, specs, offB)

        fused_state = None
        final_done = False
        for j in range(n_attn):
            li = 2 + j
            lam_init = 0.8 - 0.6 * math.exp(-0.3 * li)
            if fused_state is not None:
                hT, offB = fused_state
            else:
                hT, offB, bufs = norm_alloc()
                rms_stage(src_x, lambda c, j=j: bng[:, j, c:c + 1], hT, bufs)
            wqv = b_w_in[j].rearrange("(c p) n -> p c n", p=128)
            specs = [(h * 128, "q", ("head2", QTd, h)) for h in range(8)] + \
                    [(1024 + h * 128, "silu", ("feat", GTd, h)) for h in range(8)]
            proj_stage(hT, wqv, specs, offB)

            Sc.barrier()
            AR.reset()
            Kh = [[AR.alloc([66, S], BF16) for i in range(2)] for b in range(2)]
            Qh = [[AR.alloc([66, S], BF16) for i in range(2)] for b in range(2)]
            Vh = [AR.alloc([128, 32, 128], BF16) for b in range(2)]
            Pt = [AR.alloc([128, 2, 512], BF16) for b in range(2)]
            abias = AR.alloc([128, 8, 35], F32)
            mdiag_f = AR.alloc([128, 8, 128], F32)
            mdiag = AR.alloc([128, 8, 128], BF16)
            r12 = AR.alloc([128, 2, 512], F32)
            o12 = AR.alloc([128, 2, 512], F32)
            ocm = AR.alloc([128, 512], F32)
            osq = AR.alloc([128, 512], BF16)
            rst = AR.alloc([128, 512], F32)
            gt = [AR.alloc([128, 512], BF16) for b in range(3)]
            yo = [AR.alloc([128, 512], BF16) for b in range(2)]
            lt = AR.alloc([128, 2, 64], F32)
            Sc.dma("sp", lambda e: e.dma_start(out=abias, in_=c_abias[:, :, :]), "ld_abias", writes=["abias"])
            Sc.dma("sp", lambda e: e.dma_start(out=mdiag_f, in_=c_mdiag[:, :, :]), "ld_mdiag", writes=["mdiag_f"])
            Sc.op("dve", lambda e: e.tensor_copy(out=mdiag, in_=mdiag_f), reads=["mdiag_f"], writes=["mdiag"])
            for i in range(2):
                Sc.op("dve", lambda e, i=i: e.tensor_tensor(out=lt[:, i, :], in0=blam[:, j, 2 * i, :], in1=blam[:, j, 2 * i + 1, :], op=ALU.mult),
                      reads=["blam"], writes=["lt"])
                Sc.op("dve", lambda e, i=i: e.reduce_sum(out=lamt[:, i:i + 1], in_=lt[:, i, :], axis=mybir.AxisListType.X),
                      reads=["lt"], writes=["lamt"])
            Sc.op("act", lambda e: e.activation(out=lamt[:, 2:4], in_=lamt[:, 0:2], func=AF.Exp), reads=["lamt"], writes=["lamt"])
            Sc.op("dve", lambda e: e.scalar_tensor_tensor(out=lamt[:, 4:5], in0=lamt[:, 3:4], scalar=-lam_init, in1=lamt[:, 2:3],
                                                          op0=ALU.add, op1=ALU.subtract), reads=["lamt"], writes=["lamt"])
            Sc.op("dve", lambda e: e.tensor_scalar(out=lamt[:, 5:6], in0=bsubg[:, j:j + 1], scalar1=1.0 - lam_init, scalar2=None, op0=ALU.mult),
                  reads=["bsubg", "lamt"], writes=["lamt"])
            psS = [psA[:, 0:1024].rearrange("p (a b) -> p a b", b=512), psA[:, 1024:2048].rearrange("p (a b) -> p a b", b=512)]
            psSn = [["psA0", "psA1"], ["psA2", "psA3"]]
            psO = psB[:, 0:1024].rearrange("p (a b) -> p a b", b=512)
            psZ = psB[:, 1024:2048].rearrange("p (a b) -> p a b", b=512)
            Vtv = Vtok.rearrange("(tb p) c -> p tb c", p=128)
            def load_head(h):
                hb = h % 2
                Kn, Qn, Vn = "Kh%d" % hb, "Qh%d" % hb, "Vh%d" % hb
                for i in range(2):
                    Sc.dma("sp", lambda e, h=h, i=i, hb=hb: e.dma_start(out=Kh[hb][i], in_=KTd[h, i, :, :]), "ld_" + Kn,
                           reads=[("dram", "KTd")], writes=[Kn])
                    Sc.dma("sp", lambda e, h=h, i=i, hb=hb: e.dma_start(out=Qh[hb][i], in_=QTd[h, i, :, :]), "ld_" + Qn,
                           reads=[("dram", "QTd")], writes=[Qn])
                Sc.dma("sp", lambda e, h=h, hb=hb: e.dma_start(out=Vh[hb], in_=Vtv[:, :, h * 128:(h + 1) * 128]), "ld_" + Vn,
                       reads=[("dram", "Vtok")], writes=[Vn])

            seq = [(h, g, kb, i) for h in range(8) for g in range(8) for kb in range(4 * g + 4) for i in range(2)]
            NB = 4
            LA = 3
            psS1 = [psA[:, b * 512:(b + 1) * 512] for b in range(NB)]
            Pt1 = [Pt[b // 2][:, b % 2, :] for b in range(NB)]
            st = {"buf": {}}

            def emit_qk(idx):
                h, g, kb, i = seq[idx]
                hb = h % 2
                Kn, Qn = "Kh%d" % hb, "Qh%d" % hb
                q0 = g * 512
                if g == 0 and kb == 0 and h == 0 and i == 0:
                    load_head(0)
                if kb == 0 and i == 0:
                    eb = (h * 8 + g) % 3
                    Sc.dma("sp", lambda e, h=h, q0=q0, eb=eb: e.dma_start(out=gt[eb], in_=GTd[h * 128:(h + 1) * 128, q0:q0 + 512]),
                           "ld_gt%d" % eb, reads=[("dram", "GTd")], writes=["gt%d" % eb])
                n0 = max(0, kb - 4 * g) * 128
                sb_ = idx % NB
                st["buf"][idx] = sb_
                Sc.op("pe", lambda e, sb_=sb_, i=i, hb=hb, kb=kb, n0=n0, q0=q0: e.matmul(
                    psS1[sb_][:, n0:512], lhsT=Kh[hb][i][:, kb * 128:(kb + 1) * 128], rhs=Qh[hb][i][:, q0 + n0:q0 + 512],
                    start=True, stop=True, skip_group_check=True), reads=[Kn, Qn], writes=["psA%d" % sb_])
                if kb - 4 * g >= 0:
                    Sc.op("pe", lambda e, sb_=sb_, n0=n0, h=h: e.matmul(
                        psS1[sb_][:, n0:n0 + 128], lhsT=ident_b[:], rhs=mdiag[:, h, :],
                        start=False, stop=True, skip_group_check=True), reads=["ident_b", "mdiag"], writes=["psA%d" % sb_])

            def emit_exp_pv(idx):
                h, g, kb, i = seq[idx]
                hb = h % 2
                Vn = "Vh%d" % hb
                nkb = 4 * g + 4
                dj = kb - 4 * g
                n0 = max(0, dj) * 128
                sb_ = st["buf"][idx]
                pS, pSn = psS1[sb_], "psA%d" % sb_
                Pb, Pn = Pt1[sb_], "Pt%d" % sb_
                Sc.op("act", lambda e, pS=pS, Pb=Pb, n0=n0, h=h, dj=dj: e.activation(
                    out=Pb[:, n0:512], in_=pS[:, n0:512], func=AF.Exp, bias=abias[:, h, dj + 31:dj + 32], scale=1.0),
                    reads=[pSn, "abias"], writes=[Pn])
                Sc.op("pe", lambda e, Pb=Pb, i=i, hb=hb, kb=kb, n0=n0, nkb=nkb: e.matmul(
                    psO[:, i, n0:512], lhsT=Vh[hb][:, kb, :], rhs=Pb[:, n0:512],
                    start=(kb == 0), stop=(kb == nkb - 1), skip_group_check=True),
                    reads=[Vn, Pn], writes=["psB%d" % i])
                Sc.op("pe", lambda e, Pb=Pb, i=i, kb=kb, n0=n0, nkb=nkb: e.matmul(
                    psZ[:, i, n0:512], lhsT=ones_b[:], rhs=Pb[:, n0:512],
                    start=(kb == 0), stop=(kb == nkb - 1), skip_group_check=True),
                    reads=["ones_b", Pn], writes=["psB%d" % (2 + i)])

            def emit_E1(h, g):
                for i in range(2):
                    Sc.op("dve", lambda e, i=i: e.tensor_copy(out=r12[:, i, :], in_=psZ[:, i, :]), reads=["psB%d" % (2 + i)], writes=["r12"])
                    Sc.op("act", lambda e, i=i: e.activation(out=o12[:, i, :], in_=psO[:, i, :], func=AF.Copy), reads=["psB%d" % i], writes=["o12"])
                Sc.op("dve", lambda e: e.reciprocal(out=r12, in_=r12), reads=["r12"], writes=["r12"])
                Sc.op("dve", lambda e: e.tensor_tensor(out=o12, in0=o12, in1=r12, op=ALU.mult),
                      reads=["o12", "r12"], writes=["o12"])
                Sc.op("dve", lambda e: e.scalar_tensor_tensor(out=ocm, in0=o12[:, 1, :], scalar=lamt[:, 4:5], in1=o12[:, 0, :],
                                                              op0=ALU.mult, op1=ALU.add), reads=["o12", "lamt"], writes=["ocm"])
                Sc.op("dve", lambda e: e.tensor_tensor(out=osq, in0=ocm, in1=ocm, op=ALU.mult), reads=["ocm"], writes=["osq"])

            def emit_E2(h, g, sb_):
                q0 = g * 512
                eb = (h * 8 + g) % 3
                pS, pSn = psS1[sb_], "psA%d" % sb_
                Sc.op("pe", lambda e, pS=pS: e.matmul(pS, lhsT=ones_b[:], rhs=osq, start=True, stop=True),
                      reads=["ones_b", "osq"], writes=[pSn])
                Sc.op("act", lambda e, pS=pS: e.activation(out=rst, in_=pS, func=AF.Ln, scale=1.0 / 128.0, bias=epsb[:]),
                      reads=[pSn, "epsb"], writes=["rst"])
                Sc.op("act", lambda e: e.activation(out=rst, in_=rst, func=AF.Exp, scale=-0.5), reads=["rst"], writes=["rst"])
                Sc.op("dve", lambda e: e.scalar_tensor_tensor(out=ocm, in0=ocm, scalar=lamt[:, 5:6], in1=rst, op0=ALU.mult, op1=ALU.mult),
                      reads=["ocm", "lamt", "rst"], writes=["ocm"])
                yb, yn = yo[eb % 2], "yo%d" % (eb % 2)
                Sc.op("dve", lambda e, yb=yb, eb=eb: e.tensor_tensor(out=yb, in0=ocm, in1=gt[eb], op=ALU.mult),
                      reads=["ocm", "gt%d" % eb], writes=[yn])
                Sc.dma("pool", lambda e, yb=yb, h=h, q0=q0: e.dma_start(out=YTd[h * 128:(h + 1) * 128, q0:q0 + 512], in_=yb),
                       "st_" + yn, reads=[yn], writes=[("dram", "YTd")])

            pending = None
            E2_DELAY = 24
            for t in range(LA):
                emit_qk(t)
            for idx in range(len(seq)):
                h, g, kb, i = seq[idx]
                if idx + LA < len(seq):
                    emit_qk(idx + LA)
                emit_exp_pv(idx)
                if g == 0 and kb == 0 and i == 0 and h + 1 < 8:
                    load_head(h + 1)
                if pending is not None and (2 * kb + i) == min(E2_DELAY, 2 * (4 * g + 4) - 1):
                    emit_E2(pending[0], pending[1], st["buf"][idx])
                    pending = None
                if kb == 4 * g + 3 and i == 1:
                    emit_E1(h, g)
                    pending = (h, g)
            emit_E2(pending[0], pending[1], 0)

            Sc.barrier()
            AR.reset()
            last = (j == n_attn - 1)
            fuse_next = (not last)
            fuse_final = (last and do_final)
            hT_next = AR.alloc([128, 8, S], BF16) if fuse_next else None
            offB_next = AR.off
            wo = AR.alloc([128, 8, DM], BF16)
            wos = [AR.alloc([128, 2, DM], F32) for i in range(2)]
            ytl = [AR.alloc([128, 8, 512], BF16) for i in range(2)]
            xres = [AR.alloc([128, 8, 512], F32) for i in range(2)]
            sqp = AR.alloc([128, 8, 512], BF16)
            rsp = [AR.alloc([128, 512], F32) for i in range(2)]
            fo = [AR.alloc([128, 8, 512], F32) for i in range(2)] if fuse_final else None
            outv = outT.rearrange("(c p) t -> p c t", p=128)
            wov = b_w_out[j].rearrange("(c p) d -> p c d", p=128)
            for k4 in range(4):
                wsb = wos[k4 % 2]
                wsn = "wos%d" % (k4 % 2)
                Sc.dma("sp", lambda e, wsb=wsb, k4=k4: e.dma_start(out=wsb, in_=wov[:, k4 * 2:(k4 + 1) * 2, :]), "ld_" + wsn, writes=[wsn])
                Sc.op("dve" if k4 % 2 == 0 else "pool", lambda e, wsb=wsb, k4=k4: e.tensor_copy(out=wo[:, k4 * 2:(k4 + 1) * 2, :], in_=wsb), reads=[wsn], writes=["wo"])
            YTv = YTd.rearrange("(c p) t -> p c t", p=128)
            srcv = src_x.rearrange("(c p) t -> p c t", p=128)
            xrv = xr.rearrange("(c p) t -> p c t", p=128)
            for tt in range(8):
                b2 = tt % 2
                tsl = slice(tt * 512, (tt + 1) * 512)
                ytb, xrb = ytl[b2], xres[b2]
                ytn, xrn = "ytl%d" % b2, "xres%d" % b2
                Sc.dma("sp", lambda e, ytb=ytb, tsl=tsl: e.dma_start(out=ytb, in_=YTv[:, :, tsl]), "ld_" + ytn,
                       reads=[("dram", "YTd")], writes=[ytn])
                Sc.dma("sp", lambda e, xrb=xrb, tsl=tsl: e.dma_start(out=xrb, in_=srcv[:, :, tsl]), "ld_" + xrn,
                       reads=[("dram", src_x.name, 2 * tt), ("dram", src_x.name, 2 * tt + 1)], writes=[xrn])
                for dc in range(8):
                    pst = psA[:, (dc % 4) * 512:(dc % 4 + 1) * 512]
                    pn = "psA%d" % (dc % 4)
                    for cc in range(8):
                        Sc.op("pe", lambda e, pst=pst, dc=dc, cc=cc, ytb=ytb: e.matmul(
                            pst, lhsT=wo[:, cc, dc * 128:(dc + 1) * 128], rhs=ytb[:, cc, :], start=(cc == 0), stop=(cc == 7)),
                            reads=["wo", ytn], writes=[pn])
                    Sc.op("dve", lambda e, pst=pst, dc=dc, xrb=xrb: e.tensor_tensor(out=xrb[:, dc, :], in0=xrb[:, dc, :], in1=pst, op=ALU.add),
                          reads=[pn, xrn], writes=[xrn])
                if not fuse_final:
                    Sc.dma("pool", lambda e, xrb=xrb, tsl=tsl: e.dma_start(out=xrv[:, :, tsl], in_=xrb), "st_" + xrn,
                           reads=[xrn], writes=[("dram", "xr", 2 * tt), ("dram", "xr", 2 * tt + 1)])
                if fuse_next or fuse_final:
                    Sc.op("act", lambda e, xrb=xrb: e.activation(out=sqp, in_=xrb, func=AF.Square), reads=[xrn], writes=["sqp"])
                    for c in range(8):
                        Sc.op("pe", lambda e, c=c: e.matmul(psB[:, 0:512], lhsT=ones_b[:], rhs=sqp[:, c, :], start=(c == 0), stop=(c == 7)),
                              reads=["sqp", "ones_b"], writes=["psB0"])
                    rs, rname = rsp[b2], "rsp%d" % b2
                    Sc.op("act", lambda e, rs=rs: e.activation(out=rs, in_=psB[:, 0:512], func=AF.Ln, scale=1.0 / DM, bias=epsb[:]),
                          reads=["psB0", "epsb"], writes=[rname])
                    Sc.op("act", lambda e, rs=rs: e.activation(out=rs, in_=rs, func=AF.Exp, scale=-0.5), reads=[rname], writes=[rname])
                    for c in range(8):
                        if fuse_next:
                            Sc.op("dve", lambda e, c=c, xrb=xrb, rs=rs, tsl=tsl: e.scalar_tensor_tensor(
                                out=hT_next[:, c, tsl], in0=xrb[:, c, :], scalar=bng[:, j + 1, c:c + 1], in1=rs, op0=ALU.mult, op1=ALU.mult),
                                reads=[xrn, rname, "normg"], writes=[("hT", tt)])
                        else:
                            Sc.op("dve", lambda e, c=c, xrb=xrb, rs=rs: e.scalar_tensor_tensor(
                                out=fo[b2][:, c, :], in0=xrb[:, c, :], scalar=fing[:, c:c + 1], in1=rs, op0=ALU.mult, op1=ALU.mult),
                                reads=[xrn, rname, "normg"], writes=["fo%d" % b2])
                    if fuse_final:
                        Sc.dma("pool", lambda e, tsl=tsl, b2=b2: e.dma_start(out=outv[:, :, tsl], in_=fo[b2]), "st_out",
                               reads=["fo%d" % b2], writes=[("dram", "outT")])
            src_x = xr
            fused_state = (hT_next, offB_next) if fuse_next else None
            if fuse_final:
                final_done = True

        final_slots = []
        if final_done:
            final_slots.append("st_out")
        if do_final and not final_done:
            Sc.barrier()
            AR.reset()
            xt_buf = [AR.alloc([128, 8, 512], F32) for i in range(2)]
            sq_buf = AR.alloc([128, 8, 512], BF16)
            rs_buf = [AR.alloc([128, 512], F32) for i in range(2)]
            fo = [AR.alloc([128, 8, 512], F32) for i in range(2)]
            srcv = src_x.rearrange("(c p) t -> p c t", p=128)
            outv = outT.rearrange("(c p) t -> p c t", p=128)
            for tt in range(8):
                xb = xt_buf[tt % 2]
                xname = "xt%d" % (tt % 2)
                tsl = slice(tt * 512, (tt + 1) * 512)
                Sc.dma("sp", lambda e, xb=xb, tsl=tsl: e.dma_start(out=xb, in_=srcv[:, :, tsl]),
                       "ld_" + xname, reads=[("dram", src_x.name, 2 * tt), ("dram", src_x.name, 2 * tt + 1)], writes=[xname])
                Sc.op("act", lambda e, xb=xb: e.activation(out=sq_buf, in_=xb, func=AF.Square), reads=[xname], writes=["sq"])
                for c in range(8):
                    Sc.op("pe", lambda e, c=c: e.matmul(psA[:, 0:512], lhsT=ones_b[:], rhs=sq_buf[:, c, :], start=(c == 0), stop=(c == 7)),
                          reads=["sq", "ones_b"], writes=["psA0"])
                rs = rs_buf[tt % 2]
                rname = "rs%d" % (tt % 2)
                Sc.op("act", lambda e, rs=rs: e.activation(out=rs, in_=psA[:, 0:512], func=AF.Ln, scale=1.0 / DM, bias=epsb[:]),
                      reads=["psA0", "epsb"], writes=[rname])
                Sc.op("act", lambda e, rs=rs: e.activation(out=rs, in_=rs, func=AF.Exp, scale=-0.5), reads=[rname], writes=[rname])
                fb, fn_ = fo[tt % 2], "fo%d" % (tt % 2)
                for c in range(8):
                    Sc.op("dve", lambda e, c=c, xb=xb, rs=rs, fb=fb: e.scalar_tensor_tensor(
                        out=fb[:, c, :], in0=xb[:, c, :], scalar=fing[:, c:c + 1], in1=rs, op0=ALU.mult, op1=ALU.mult),
                        reads=[xname, rname, "normg"], writes=[fn_])
                Sc.dma("pool", lambda e, fb=fb, tsl=tsl: e.dma_start(out=outv[:, :, tsl], in_=fb), "st_out",
                       reads=[fn_], writes=[("dram", "outT")])
            final_slots.append("st_out")

        if dbg_out:
            Sc.barrier()
            for name, (srcname, shape, dt) in dbg_out.items():
                o = P.dram_out(name, shape, dt)
                src = {"zs": zs, "Xtok": Xtok, "Btok": Btok, "BTd": BTd, "CTd": CTd, "yTd": yTd, "xr": xr,
                       "KTd": KTd, "QTd": QTd, "Vtok": Vtok, "GTd": GTd, "YTd": YTd}.get(srcname)
                if src is not None:
                    Sc.dma("sp", lambda e, o=o, src=src: e.dma_start(out=o, in_=src), "dbg_" + name,
                           reads=[("dram", src.name)] + [("dram", src.name, t) for t in range(16)], writes=[("dram", name)])
                else:
                    sbt = {"TMtok": TMtok, "Ebc": Ebc, "AL": AL}[srcname]
                    Sc.dma("sp", lambda e, o=o, sbt=sbt: e.dma_start(out=o, in_=sbt[:]), "dbg_" + name,
                           reads=[srcname], writes=[("dram", name)])
                final_slots.append("dbg_" + name)

        Sc.finalize()
        sems = {}
        for e in Sc.ENG:
            sems[e] = es.enter_context(nc.semaphore("s_" + e))
        dsems = {}
        for i, s_ in enumerate(Sc.slot_order):
            dsems[s_] = es.enter_context(nc.semaphore("d%d" % i))
        block = es.enter_context(nc.Block())
        Sc.emit(nc, block, sems, dsems, final_slots)
    return nc


def host_constants():
    ident = np.eye(128, dtype=np.float32)
    ind = np.zeros((128, 32), np.float32)
    ind[np.arange(128), np.arange(128) % 32] = 1.0
    s = np.arange(64)[:, None]; l = np.arange(64)[None, :]
    negm = np.zeros((128, 64), np.float32)
    negm[64:] = np.where(l < s, NEG, 0.0)
    segm = np.ones((128, S), np.float32)
    segm[:, ::64] = 0.0
    onid = np.ones((128, 64), np.float32)
    onid[64:] = np.eye(64, dtype=np.float32)
    bf = ml_dtypes.bfloat16
    slopes = 2.0 ** (-8.0 * np.arange(1, 9) / 8.0)
    q = np.arange(S)
    qm = q % 512
    qaug = np.stack([-(slopes[:, None]) * (qm & ~1)[None, :], -(slopes[:, None]) * (qm & 1)[None, :]], axis=1)
    kaug = np.ones((2, S), np.float32)
    kl = np.arange(128)[:, None, None]
    d = (np.arange(35) - 31)[None, None, :]
    abias = slopes[None, :, None] * (kl + 128.0 * d)
    k = np.arange(128)[:, None]; qq = np.arange(128)[None, :]
    kc, qc = k // 64, qq // 64
    md = np.zeros((128, 8, 128), np.float64)
    for h in range(8):
        m = np.where(kc > qc, NEG, np.where((kc == qc) & (k > qq), -2.0 * slopes[h] * (k - qq), 0.0))
        md[:, h, :] = m
    return {"c_ident": ident, "c_ind": ind, "c_negm": negm, "c_segm": segm.astype(bf), "c_onid": onid,
            "c_kaug": kaug.astype(bf), "c_qaug": qaug.astype(np.float32).astype(bf),
            "c_abias": abias.astype(np.float32), "c_mdiag": md.astype(np.float32)}


def host_layout(inp):
    f = np.float32
    d = {}
    d["a_w_in"] = np.ascontiguousarray(inp["a_w_in"], dtype=f)
    d["a_w_out"] = np.ascontiguousarray(inp["a_w_out"], dtype=f)
    d["a_norm_g"] = np.ascontiguousarray(inp["a_norm_g"].reshape(2, 8, 128).transpose(2, 0, 1), dtype=f)
    d["a_conv_w"] = np.ascontiguousarray(inp["a_conv_w"].reshape(2, 4, 32, 128).transpose(3, 0, 2, 1), dtype=f)
    d["a_conv_b"] = np.ascontiguousarray(inp["a_conv_b"].reshape(2, 32, 128).transpose(2, 0, 1), dtype=f)
    hp = np.stack([inp["a_dt_bias"], inp["a_a_log"]], axis=-1)
    d["a_hp"] = np.ascontiguousarray(np.tile(hp.transpose(1, 0, 2), (4, 1, 1)), dtype=f)
    d["a_dskip"] = np.ascontiguousarray(np.broadcast_to(inp["a_d_skip"][None], (128, 2, 32)), dtype=f)
    d["a_gng"] = np.ascontiguousarray(inp["a_gate_norm_g"].reshape(2, 16, 128).transpose(2, 0, 1), dtype=f)
    d["kv_norm_g"] = np.ascontiguousarray(inp["kv_norm_g"].reshape(8, 128).T, dtype=f)
    d["w_kv"] = np.ascontiguousarray(inp["w_kv"], dtype=f)
    d["b_norm_g"] = np.ascontiguousarray(inp["b_norm_g"].reshape(2, 8, 128).transpose(2, 0, 1), dtype=f)
    d["b_w_in"] = np.ascontiguousarray(inp["b_w_in"], dtype=f)
    d["b_w_out"] = np.ascontiguousarray(inp["b_w_out"], dtype=f)
    d["b_lam"] = np.ascontiguousarray(np.broadcast_to(inp["b_lambda"][None], (128, 2, 4, 64)), dtype=f)
    d["b_subg"] = np.ascontiguousarray(inp["b_sub_g"].T, dtype=f)
    d["final_g"] = np.ascontiguousarray(inp["final_norm_g"].reshape(8, 128).T, dtype=f)
    d.update(host_constants())
    return d


_NC_CACHE = {}


def kernel(**inputs):
    inp = {k: np.asarray(v) for k, v in inputs.items()}
    if "nc" not in _NC_CACHE:
        _NC_CACHE["nc"] = build_program()
    nc = _NC_CACHE["nc"]
    hl = host_layout(inp)
    x = inp["x"].astype(np.float32)
    maps = []
    for b in range(8):
        m = dict(hl)
        m["xT"] = np.ascontiguousarray(x[b].T)
        maps.append(m)
    res = run_bass_kernel_spmd(nc, maps, core_ids=list(range(8)))
    out = np.stack([np.ascontiguousarray(res.results[b]["outT"].T) for b in range(8)], axis=0)
    return out.astype(np.float32)
```
